# Optimizing a Trainium2 kernel written in Bass

```python
import math
import jax, jax.numpy as jnp
from jax import lax
import numpy as np

D_MODEL = 2048
BATCH = 4
SEQ = 4096
DEPTH = 4

CHUNK = 64
N_MIXERS = 2
N_A = (DEPTH + 1) // 2
N_B = DEPTH // 2

GM_BLOCK = 128
GM_HALF = 3 * D_MODEL
GM_GROUPS = 8
GM_GROUP_DIM = GM_HALF // GM_GROUPS

MLA_HEADS = 16
Q_RANK = 512
KV_RANK = 512
NOPE_DIM = 128
ROPE_DIM = 64
V_DIM = 128
ROPE_THETA = 10000.0
Q_BLOCK = 128
SM_SCALE = (NOPE_DIM + ROPE_DIM) ** -0.5

D_FF = 5504
CONV_W = 3

ALPHA = (2 * DEPTH) ** 0.25
BETA = (8 * DEPTH) ** -0.25
LN_EPS = 1e-5
RMS_EPS = 1e-6

kernel_name = "interleaved_gmlp_mla_convffn_deepnorm"


def layer_norm(x, g, b):
    xf = x.astype(jnp.float32)
    mu = jnp.mean(xf, axis=-1, keepdims=True)
    var = jnp.mean(jnp.square(xf - mu), axis=-1, keepdims=True)
    y = (xf - mu) * lax.rsqrt(var + LN_EPS)
    return (y * g.astype(jnp.float32) + b.astype(jnp.float32)).astype(x.dtype)


def rms_norm(x, g):
    xf = x.astype(jnp.float32)
    y = xf * lax.rsqrt(jnp.mean(jnp.square(xf), axis=-1, keepdims=True) + RMS_EPS)
    return (y * g.astype(jnp.float32)).astype(x.dtype)


def rope_tables(seq_len):
    half = ROPE_DIM // 2
    inv_freq = ROPE_THETA ** (-jnp.arange(half, dtype=jnp.float32) / half)
    pos = jnp.arange(seq_len, dtype=jnp.float32)
    ang = pos[:, None] * inv_freq[None, :]
    return jnp.cos(ang), jnp.sin(ang)


def apply_rope(x, cos, sin):
    x1, x2 = jnp.split(x, 2, axis=-1)
    cos = cos.astype(x.dtype)
    sin = sin.astype(x.dtype)
    return jnp.concatenate([x1 * cos - x2 * sin, x2 * cos + x1 * sin], axis=-1)


def gmlp_mixer(x, w_in, ln_g, ln_b, w_s, b_s, w_out):
    B, S, _ = x.shape
    z = jax.nn.gelu(x @ w_in)
    u, v = jnp.split(z, 2, axis=-1)
    v = layer_norm(v, ln_g, ln_b)
    v = v.reshape(B, S // GM_BLOCK, GM_BLOCK, GM_GROUPS, GM_GROUP_DIM)
    idx = jnp.arange(GM_BLOCK) // CHUNK
    mask = (idx[None, :] <= idx[:, None]).astype(w_s.dtype)
    w = w_s * mask[None]
    s = jnp.einsum('gij,bnjgd->bnigd', w, v) + jnp.transpose(b_s)[None, None, :, :, None]
    return (u * s.reshape(B, S, GM_HALF)) @ w_out


def chunk_causal_attention(q_nope, q_rope, k_nope, k_rope, v):
    S = q_nope.shape[1]
    outs = []
    for qb in range(S // Q_BLOCK):
        q0 = qb * Q_BLOCK
        k_end = q0 + Q_BLOCK
        s = (jnp.einsum('bqhd,bkhd->bhqk', q_nope[:, q0:k_end], k_nope[:, :k_end])
             + jnp.einsum('bqhr,bkr->bhqk', q_rope[:, q0:k_end], k_rope[:, :k_end]))
        s = s.astype(jnp.float32) * SM_SCALE
        qpos = jnp.arange(q0, k_end)
        kpos = jnp.arange(k_end)
        allowed = (kpos[None, :] // CHUNK) <= (qpos[:, None] // CHUNK)
        s = jnp.where(allowed[None, None], s, -jnp.inf)
        p = jax.nn.softmax(s, axis=-1).astype(v.dtype)
        outs.append(jnp.einsum('bhqk,bkhd->bqhd', p, v[:, :k_end]))
    return jnp.concatenate(outs, axis=1)


def mla_mixer(x, w_in, q_norm_g, kv_norm_g, w_q_b, w_kv_b, w_out, cos, sin):
    B, S, _ = x.shape
    h = x @ w_in
    c_q, c_kv, k_rope = jnp.split(h, [Q_RANK, Q_RANK + KV_RANK], axis=-1)
    q = (rms_norm(c_q, q_norm_g) @ w_q_b).reshape(B, S, MLA_HEADS, NOPE_DIM + ROPE_DIM)
    q_nope, q_rope = jnp.split(q, [NOPE_DIM], axis=-1)
    q_rope = apply_rope(q_rope, cos[:, None, :], sin[:, None, :])
    k_rope = apply_rope(k_rope, cos, sin)
    kv = (rms_norm(c_kv, kv_norm_g) @ w_kv_b).reshape(B, S, MLA_HEADS, NOPE_DIM + V_DIM)
    k_nope, v = jnp.split(kv, [NOPE_DIM], axis=-1)
    o = chunk_causal_attention(q_nope, q_rope, k_nope, k_rope, v)
    return o.reshape(B, S, MLA_HEADS * V_DIM) @ w_out


def conv_ffn(x, w_up, conv_w, conv_b, w_down):
    h = x @ w_up
    C = h.shape[-1]
    h = lax.conv_general_dilated(h, conv_w[:, None, :], window_strides=(1,),
                                 padding=[(CONV_W - 1, 0)],
                                 dimension_numbers=('NWC', 'WIO', 'NWC'),
                                 feature_group_count=C) + conv_b
    a, g = jnp.split(h, 2, axis=-1)
    return (jax.nn.silu(g) * a) @ w_down


def setup_inputs(seed: int = 0) -> dict:
    key = jax.random.key(seed)
    ks = jax.random.split(key, 24)

    def nrm(k, shape, scale):
        return jax.random.normal(k, shape, jnp.float32) * scale

    x = nrm(ks[0], (BATCH, SEQ, D_MODEL), 1.0)
    gm_w_in = nrm(ks[1], (N_A, D_MODEL, 2 * GM_HALF), D_MODEL ** -0.5)
    gm_ln_g = 1.0 + nrm(ks[2], (N_A, GM_HALF), 0.02)
    gm_ln_b = nrm(ks[3], (N_A, GM_HALF), 0.02)
    gm_w_s = nrm(ks[4], (N_A, GM_GROUPS, GM_BLOCK, GM_BLOCK), GM_BLOCK ** -0.5)
    gm_b_s = 1.0 + nrm(ks[5], (N_A, GM_GROUPS, GM_BLOCK), 0.1)
    gm_w_out = nrm(ks[6], (N_A, GM_HALF, D_MODEL), GM_HALF ** -0.5 * BETA)
    mla_w_in = nrm(ks[7], (N_B, D_MODEL, Q_RANK + KV_RANK + ROPE_DIM), D_MODEL ** -0.5)
    mla_q_norm_g = 1.0 + nrm(ks[8], (N_B, Q_RANK), 0.02)
    mla_kv_norm_g = 1.0 + nrm(ks[9], (N_B, KV_RANK), 0.02)
    mla_w_q_b = nrm(ks[10], (N_B, Q_RANK, MLA_HEADS * (NOPE_DIM + ROPE_DIM)), Q_RANK ** -0.5)
    mla_w_kv_b = nrm(ks[11], (N_B, KV_RANK, MLA_HEADS * (NOPE_DIM + V_DIM)), KV_RANK ** -0.5)
    mla_w_out = nrm(ks[12], (N_B, MLA_HEADS * V_DIM, D_MODEL), (MLA_HEADS * V_DIM) ** -0.5 * BETA)
    ffn_w_up = nrm(ks[13], (DEPTH, D_MODEL, 2 * D_FF), D_MODEL ** -0.5)
    ffn_conv_w = nrm(ks[14], (DEPTH, CONV_W, 2 * D_FF), CONV_W ** -0.5)
    ffn_conv_b = nrm(ks[15], (DEPTH, 2 * D_FF), 0.02)
    ffn_w_down = nrm(ks[16], (DEPTH, D_FF, D_MODEL), D_FF ** -0.5 * BETA)
    ln_mix_g = 1.0 + nrm(ks[17], (DEPTH, D_MODEL), 0.02)
    ln_mix_b = nrm(ks[18], (DEPTH, D_MODEL), 0.02)
    ln_ffn_g = 1.0 + nrm(ks[19], (DEPTH, D_MODEL), 0.02)
    ln_ffn_b = nrm(ks[20], (DEPTH, D_MODEL), 0.02)
    return {
        "x": x,
        "gm_w_in": gm_w_in, "gm_ln_g": gm_ln_g, "gm_ln_b": gm_ln_b,
        "gm_w_s": gm_w_s, "gm_b_s": gm_b_s, "gm_w_out": gm_w_out,
        "mla_w_in": mla_w_in, "mla_q_norm_g": mla_q_norm_g, "mla_kv_norm_g": mla_kv_norm_g,
        "mla_w_q_b": mla_w_q_b, "mla_w_kv_b": mla_w_kv_b, "mla_w_out": mla_w_out,
        "ffn_w_up": ffn_w_up, "ffn_conv_w": ffn_conv_w, "ffn_conv_b": ffn_conv_b,
        "ffn_w_down": ffn_w_down,
        "ln_mix_g": ln_mix_g, "ln_mix_b": ln_mix_b, "ln_ffn_g": ln_ffn_g, "ln_ffn_b": ln_ffn_b,
    }


def reference(x, gm_w_in, gm_ln_g, gm_ln_b, gm_w_s, gm_b_s, gm_w_out,
              mla_w_in, mla_q_norm_g, mla_kv_norm_g, mla_w_q_b, mla_w_kv_b, mla_w_out,
              ffn_w_up, ffn_conv_w, ffn_conv_b, ffn_w_down,
              ln_mix_g, ln_mix_b, ln_ffn_g, ln_ffn_b):
    cos, sin = rope_tables(x.shape[1])
    for i in range(DEPTH):
        slot = i // N_MIXERS
        if i % N_MIXERS == 0:
            m = gmlp_mixer(x, gm_w_in[slot], gm_ln_g[slot], gm_ln_b[slot],
                           gm_w_s[slot], gm_b_s[slot], gm_w_out[slot])
        else:
            m = mla_mixer(x, mla_w_in[slot], mla_q_norm_g[slot], mla_kv_norm_g[slot],
                          mla_w_q_b[slot], mla_w_kv_b[slot], mla_w_out[slot], cos, sin)
        x = layer_norm(ALPHA * x + m, ln_mix_g[i], ln_mix_b[i])
        f = conv_ffn(x, ffn_w_up[i], ffn_conv_w[i], ffn_conv_b[i], ffn_w_down[i])
        x = layer_norm(ALPHA * x + f, ln_ffn_g[i], ln_ffn_b[i])
    return x
```

```python
import numpy as np
import concourse.bass as bass
import concourse.mybir as mybir
from concourse.bass_utils import run_bass_kernel_spmd

F32 = mybir.dt.float32
BF16 = mybir.dt.bfloat16
AF = mybir.ActivationFunctionType
ALU = mybir.AluOpType
AX = mybir.AxisListType

PHYS = ["tensor", "vector", "scalar", "gpsimd", "sync"]


class Res:
    __slots__ = ("name", "w", "rs")

    def __init__(self, name=""):
        self.name = name
        self.w = None
        self.rs = []


class Op:
    __slots__ = ("eng", "fn", "deps", "semkey", "sem", "inc", "signal", "signo")


class Prog:
    def __init__(self, nc):
        self.nc = nc
        self.ops = {e: [] for e in PHYS}
        self.sems = {}
        self.phase = 0
        self.nops = 0

    def new_phase(self):
        self.phase += 1
        self.sems = {}

    def _sem(self, key):
        if key not in self.sems:
            self.sems[key] = self.nc.alloc_semaphore(name=f"s{self.phase}_{key}")
        return self.sems[key]

    def op(self, eng, fn, reads=(), writes=(), dma=False, semkey=None):
        o = Op()
        o.eng = eng
        o.fn = fn
        o.inc = 16 if dma else 1
        o.semkey = semkey or (eng + ("_dma" if dma else ""))
        o.sem = self._sem(o.semkey)
        o.signal = bool(dma)
        o.signo = 0
        deps = {}

        def add(d, kind):
            if d is None or d is o:
                return
            if d.eng == eng and d.inc == 1 and not dma:
                if eng == "tensor" or kind != "RAW":
                    return
            deps[id(d)] = d
            d.signal = True

        reads = [r.res if hasattr(r, "res") else r for r in reads]
        writes = [r.res if hasattr(r, "res") else r for r in writes]
        for r in reads:
            add(r.w, "RAW")
        for r in writes:
            add(r.w, "WAW")
            for rd in r.rs:
                add(rd, "WAR")
        for r in reads:
            r.rs.append(o)
        for r in writes:
            r.w = o
            r.rs = []
        o.deps = list(deps.values())
        self.ops[eng].append(o)
        self.nops += 1
        return o

    def dma(self, eng, out, in_, reads=(), writes=(), semkey=None):
        return self.op(eng, lambda e: e.dma_start(out=out, in_=in_), reads, writes,
                       dma=True, semkey=semkey)

    def barrier_wait(self, eng, ops):
        o = Op()
        o.eng = eng
        o.fn = None
        o.inc = 1
        o.semkey = eng
        o.sem = self._sem(eng)
        o.signal = False
        o.signo = 0
        for d in ops:
            d.signal = True
        o.deps = list(ops)
        self.ops[eng].append(o)
        return o

    def emit(self):
        cnt = {}
        for e in PHYS:
            for o in self.ops[e]:
                if o.signal:
                    k = id(o.sem)
                    cnt[k] = cnt.get(k, 0) + o.inc
                    o.signo = cnt[k]
        self.maxsem = max(cnt.values()) if cnt else 0
        nc = self.nc
        nwaits = [0]
        with nc.Block() as block:
            for e in PHYS:
                if not self.ops[e]:
                    continue

                def body(engobj, e=e):
                    known = {}
                    for o in self.ops[e]:
                        need = {}
                        for d in o.deps:
                            k = id(d.sem)
                            if k not in need or need[k][1] < d.signo:
                                need[k] = (d.sem, d.signo)
                        for k, (sem, v) in need.items():
                            if known.get(k, 0) < v:
                                engobj.wait_ge(sem, v)
                                known[k] = v
                                nwaits[0] += 1
                        if o.fn is not None:
                            ins = o.fn(engobj)
                            if o.signal:
                                ins.then_inc(o.sem, o.inc)

                getattr(block, e)(body)
        self.nwaits = nwaits[0]


class Buf:
    def __init__(self, t, name=""):
        self.t = t
        self.res = Res(name)
        self.w = None

    def __getitem__(self, k):
        return self.t[k]


LN_EPS = 1e-5


def cdiv(a, b):
    return (a + b - 1) // b


class Rot:
    def __init__(self, bufs):
        self.bufs = bufs
        self.i = 0

    def next(self):
        b = self.bufs[self.i % len(self.bufs)]
        self.i += 1
        return b


def ln_feature_major(P, nc, C, r, rres, DC, TT, ones, psS1, psS2, lng, lnb, sq_rot, st, D):
    mean, msq, var, rstd = st
    P.op("vector", lambda e: e.tensor_scalar(out=mean[:, :], in0=psS1[:, :], scalar1=1.0 / D, scalar2=None, op0=ALU.mult),
         reads=[psS1], writes=[mean])
    P.op("vector", lambda e: e.tensor_tensor(out=msq[:, :], in0=mean[:, :], in1=mean[:, :], op=ALU.mult),
         reads=[mean], writes=[msq])
    P.op("vector", lambda e: e.scalar_tensor_tensor(out=var[:, :], in0=psS2[:, :], scalar=1.0 / D, in1=msq[:, :],
                                                    op0=ALU.mult, op1=ALU.subtract),
         reads=[psS2, msq], writes=[var])
    P.op("vector", lambda e: e.tensor_scalar(out=var[:, :], in0=var[:, :], scalar1=LN_EPS, scalar2=None, op0=ALU.add),
         reads=[var], writes=[var])
    P.op("scalar", lambda e: e.activation(out=msq[:, :], in_=var[:, :], func=AF.Sqrt),
         reads=[var], writes=[msq])
    P.op("vector", lambda e: e.reciprocal(out=rstd[:, :], in_=msq[:, :]), reads=[msq], writes=[rstd])
    for d in range(DC):
        P.op("vector", lambda e, d=d: e.tensor_tensor(out=r[:, d, :], in0=r[:, d, :], in1=mean[:, :], op=ALU.subtract),
             reads=[rres[d], mean], writes=[rres[d]])
        P.op("gpsimd", lambda e, d=d: e.tensor_tensor(out=r[:, d, :], in0=r[:, d, :], in1=rstd[:, :], op=ALU.mult),
             reads=[rres[d], rstd], writes=[rres[d]])
        P.op("scalar", lambda e, d=d: e.activation(out=r[:, d, :], in_=r[:, d, :], func=AF.Identity,
                                                   bias=lnb[:, d:d + 1], scale=lng[:, d:d + 1]),
             reads=[rres[d]], writes=[rres[d]])


def build_ffn(nc, P, io, D, DFF, T, TT=512, alpha=1.0, ncolblk=512, ndblk=256):
    DC = D // 128
    HC = DFF // 128
    NF = 2 * HC
    NT = T // TT
    xT, xh, w_up, w_down, yT = io["xT"], io["xh"], io["w_up"], io["w_down"], io["yT"]

    A = nc.alloc_sbuf_tensor
    cw = Buf(A("f_cw", [128, 3, NF], F32))
    cb = Buf(A("f_cb", [128, NF], F32))
    lng = Buf(A("f_lng", [128, DC], F32))
    lnb = Buf(A("f_lnb", [128, DC], F32))
    ones = Buf(A("f_ones", [128, 128], F32))
    xs = Buf(A("f_xs", [128, DC, TT], F32))
    xs_res = [Res() for _ in range(DC)]
    xb = A("f_xb", [128, DC, TT], BF16)
    xb_res = [Res() for _ in range(DC)]
    xhs = Buf(A("f_xhs", [128, DC, 2], F32))
    xhb = Buf(A("f_xhb", [128, DC, 2], BF16))
    hid = A("f_hid", [128, HC, TT], BF16)
    hid_res = [Res() for _ in range(HC)]
    halo = A("f_halo", [128, NF, 2], F32)
    halo_res = [Res() for _ in range(NF)]
    wa = Rot([Buf(A(f"f_wa{i}", [128, DC, ncolblk], BF16)) for i in range(2)])
    wg = Rot([Buf(A(f"f_wg{i}", [128, DC, ncolblk], BF16)) for i in range(2)])
    wd = Rot([Buf(A(f"f_wd{i}", [128, HC, ndblk], BF16)) for i in range(2)])
    hbuf = Rot([Buf(A(f"f_hbuf{i}", [128, TT + 2], F32)) for i in range(4)])
    ybuf = Rot([Buf(A(f"f_y{i}", [128, TT], F32)) for i in range(4)])
    sgbuf = Rot([Buf(A(f"f_sg{i}", [128, TT], F32)) for i in range(2)])
    sqbuf = Rot([Buf(A(f"f_sq{i}", [128, TT], F32)) for i in range(2)])
    st = [Buf(A(f"f_st{i}", [128, TT], F32)) for i in range(4)]
    PS = nc.alloc_psum_tensor
    pool = Rot([Buf(PS(f"f_ps{i}", [128, TT], F32)) for i in range(5)])
    psH = PS("f_psH", [128, 512], F32)
    psH_res = [Res() for _ in range(HC)]
    psS1 = Buf(PS("f_psS1", [128, TT], F32))
    psS2 = Buf(PS("f_psS2", [128, TT], F32))

    P.dma("sync", cw[:, :, :], io["cw"].rearrange("p (k c) -> p k c", k=3), writes=[cw])
    P.dma("sync", cb[:, :], io["cb"], writes=[cb])
    P.dma("sync", lng[:, :], io["lng"], writes=[lng])
    P.dma("sync", lnb[:, :], io["lnb"], writes=[lnb])
    P.op("vector", lambda e: e.memset(ones[:, :], 1.0), writes=[ones])
    P.dma("sync", xhs[:, :, :], xh.rearrange("(c p) n -> p c n", p=128), writes=[xhs])
    P.op("vector", lambda e: e.tensor_copy(out=xhb[:, :, :], in_=xhs[:, :, :]), reads=[xhs], writes=[xhb])

    out_ops = []
    stages = []

    def run_stages():
        pend = None
        for i, (ld, comp) in enumerate(stages):
            cur = pend if pend is not None else ld()
            pend = stages[i + 1][0]() if i + 1 < len(stages) else None
            comp(cur)
        stages.clear()

    for tt in range(NT):
        t0 = tt * TT
        P.dma("sync", xs[:, :, :], xT[:, t0:t0 + TT].rearrange("(c p) n -> p c n", p=128), writes=xs_res)
        for d in range(DC):
            eng = "vector" if d % 2 == 0 else "gpsimd"
            P.op(eng, lambda e, d=d: e.tensor_copy(out=xb[:, d, :], in_=xs[:, d, :]), reads=[xs_res[d]], writes=[xb_res[d]])
        nblk = cdiv(HC, ncolblk // 128)
        for blk in range(nblk):
            c0 = blk * (ncolblk // 128)
            nch = min(ncolblk // 128, HC - c0)

            def ld_up(c0=c0, nch=nch):
                wab, wgb = wa.next(), wg.next()
                P.dma("gpsimd", wab[:, :, :nch * 128],
                      w_up[:, c0 * 128:(c0 + nch) * 128].rearrange("(c p) n -> p c n", p=128), writes=[wab])
                P.dma("gpsimd", wgb[:, :, :nch * 128],
                      w_up[:, DFF + c0 * 128:DFF + (c0 + nch) * 128].rearrange("(c p) n -> p c n", p=128), writes=[wgb])
                return wab, wgb

            def comp_up(bufs, c0=c0, nch=nch, tt=tt):
                wab, wgb = bufs
                for cc in range(nch):
                    c = c0 + cc
                    pa, pg = pool.next(), pool.next()
                    for (pp, wb_) in ((pa, wab), (pg, wgb)):
                        for kc in range(DC):
                            P.op("tensor", lambda e, pp=pp, wb_=wb_, kc=kc, cc=cc: e.matmul(
                                pp[:, :], lhsT=wb_[:, kc, cc * 128:(cc + 1) * 128], rhs=xb[:, kc, :],
                                start=(kc == 0), stop=(kc == DC - 1)), reads=[wb_, xb_res[kc]], writes=[pp])
                    if tt == 0:
                        for hi, wb_ in enumerate((wab, wgb)):
                            for kc in range(DC):
                                P.op("tensor", lambda e, hi=hi, wb_=wb_, kc=kc, cc=cc, c=c: e.matmul(
                                    psH[:, c * 4 + hi * 2:c * 4 + hi * 2 + 2], lhsT=wb_[:, kc, cc * 128:(cc + 1) * 128],
                                    rhs=xhb[:, kc, :], start=(kc == 0), stop=(kc == DC - 1)),
                                    reads=[wb_, xhb], writes=[psH_res[c]])
                    ys = []
                    for hi, pp in enumerate((pa, pg)):
                        fc = c + hi * HC
                        hb = hbuf.next()
                        y = ybuf.next()
                        P.op("scalar", lambda e, hb=hb, pp=pp: e.activation(out=hb[:, 2:TT + 2], in_=pp[:, :], func=AF.Copy),
                             reads=[pp], writes=[hb])
                        if tt == 0:
                            P.op("vector", lambda e, hb=hb, c=c, hi=hi: e.tensor_copy(
                                out=hb[:, 0:2], in_=psH[:, c * 4 + hi * 2:c * 4 + hi * 2 + 2]),
                                reads=[psH_res[c]], writes=[hb])
                        else:
                            P.op("vector", lambda e, hb=hb, fc=fc: e.tensor_copy(out=hb[:, 0:2], in_=halo[:, fc, :]),
                                 reads=[halo_res[fc]], writes=[hb])
                        P.op("vector", lambda e, hb=hb, fc=fc: e.tensor_copy(out=halo[:, fc, :], in_=hb[:, TT:TT + 2]),
                             reads=[hb], writes=[halo_res[fc]])
                        P.op("scalar", lambda e, y=y, pp=pp, fc=fc: e.activation(
                            out=y[:, :], in_=pp[:, :], func=AF.Identity, bias=cb[:, fc:fc + 1], scale=cw[:, 2, fc:fc + 1]),
                            reads=[pp, cb, cw], writes=[y])
                        P.op("vector", lambda e, y=y, hb=hb, fc=fc: e.scalar_tensor_tensor(
                            out=y[:, :], in0=hb[:, 1:TT + 1], scalar=cw[:, 1, fc:fc + 1], in1=y[:, :],
                            op0=ALU.mult, op1=ALU.add), reads=[hb, y, cw], writes=[y])
                        P.op("vector", lambda e, y=y, hb=hb, fc=fc: e.scalar_tensor_tensor(
                            out=y[:, :], in0=hb[:, 0:TT], scalar=cw[:, 0, fc:fc + 1], in1=y[:, :],
                            op0=ALU.mult, op1=ALU.add), reads=[hb, y, cw], writes=[y])
                        ys.append(y)
                    ya, yg = ys
                    sg = sgbuf.next()
                    P.op("scalar", lambda e, sg=sg, yg=yg: e.activation(out=sg[:, :], in_=yg[:, :], func=AF.Silu),
                         reads=[yg], writes=[sg])
                    P.op("vector", lambda e, sg=sg, ya=ya, c=c: e.tensor_tensor(
                        out=hid[:, c, :], in0=sg[:, :], in1=ya[:, :], op=ALU.mult), reads=[sg, ya], writes=[hid_res[c]])

            stages.append((ld_up, comp_up))
        ndb = D // ndblk
        for db in range(ndb):
            def ld_dn(db=db):
                wdb = wd.next()
                P.dma("gpsimd", wdb[:, :, :], w_down[:, db * ndblk:(db + 1) * ndblk].rearrange("(c p) n -> p c n", p=128),
                      writes=[wdb])
                return wdb

            def comp_dn(wdb, db=db):
                for dc in range(ndblk // 128):
                    d = db * (ndblk // 128) + dc
                    po = pool.next()
                    for c in range(HC):
                        P.op("tensor", lambda e, po=po, wdb=wdb, c=c, dc=dc: e.matmul(
                            po[:, :], lhsT=wdb[:, c, dc * 128:(dc + 1) * 128], rhs=hid[:, c, :],
                            start=(c == 0), stop=(c == HC - 1)), reads=[wdb, hid_res[c]], writes=[po])
                    P.op("vector", lambda e, po=po, d=d: e.scalar_tensor_tensor(
                        out=xs[:, d, :], in0=xs[:, d, :], scalar=float(alpha), in1=po[:, :], op0=ALU.mult, op1=ALU.add),
                        reads=[xs_res[d], po], writes=[xs_res[d]])
                    sq = sqbuf.next()
                    P.op("scalar", lambda e, sq=sq, d=d: e.activation(out=sq[:, :], in_=xs[:, d, :], func=AF.Square),
                         reads=[xs_res[d]], writes=[sq])
                    extra = psH_res if d == 0 else []
                    P.op("tensor", lambda e, d=d: e.matmul(psS1[:, :], lhsT=ones[:, :], rhs=xs[:, d, :],
                                                           start=(d == 0), stop=(d == DC - 1)),
                         reads=[ones, xs_res[d]], writes=[psS1] + extra)
                    P.op("tensor", lambda e, d=d, sq=sq: e.matmul(psS2[:, :], lhsT=ones[:, :], rhs=sq[:, :],
                                                                 start=(d == 0), stop=(d == DC - 1)),
                         reads=[ones, sq], writes=[psS2])

            stages.append((ld_dn, comp_dn))
        run_stages()
        ln_feature_major(P, nc, None, xs, xs_res, DC, TT, ones, psS1, psS2, lng, lnb, sqbuf, st, D)
        o = P.dma("sync", yT[:, t0:t0 + TT].rearrange("(c p) n -> p c n", p=128), xs[:, :, :], reads=xs_res)
        out_ops.append(o)
    return out_ops


def build_gmlp(nc, P, io, D, GH, G, T, TT=512, alpha=1.0, ndblk=128):
    DC = D // 128
    VC = GH // 128
    CPG = VC // G
    NT = T // TT
    NW = TT // 128
    NVB = GH // 512
    xT, w_in, w_out, yT = io["xT"], io["w_in"], io["w_out"], io["yT"]
    A = nc.alloc_sbuf_tensor
    PS = nc.alloc_psum_tensor

    gb = Buf(A("g_gb", [128, 2, VC], F32))
    lng = Buf(A("g_lng", [128, DC], F32))
    lnb = Buf(A("g_lnb", [128, DC], F32))
    ones = Buf(A("g_ones", [128, 128], F32))
    wsf = Buf(A("g_wsf", [128, G, 128], F32))
    wsb = Buf(A("g_wsb", [128, G, 128], BF16))
    rsb = Buf(A("g_rsb", [128, G, 128], F32))
    bsb = Buf(A("g_bsb", [128, G, 128], F32))
    xs = A("g_xs", [128, DC, TT], F32)
    xs_res = [Res() for _ in range(DC)]
    xb = A("g_xb", [128, DC, TT], BF16)
    xb_res = [Res() for _ in range(DC)]
    vt = A("g_vt", [128, VC, NW, 128], BF16)
    vt_res = [Res() for _ in range(VC)]
    wio = Rot([Buf(A(f"g_wio{i}", [128, DC, 512], BF16)) for i in range(2)])
    wo = Rot([Buf(A(f"g_wo{i}", [128, VC, ndblk], BF16)) for i in range(2)])
    ug = Rot([Buf(A(f"g_ug{i}", [128, TT], F32)) for i in range(2)])
    sv = Rot([Buf(A(f"g_sv{i}", [128, TT], F32)) for i in range(2)])
    ad = Rot([Buf(A(f"g_ad{i}", [128, 128], F32)) for i in range(2)])
    junk = Rot([Buf(A(f"g_junk{i}", [128, 512], BF16)) for i in range(2)])
    sqbuf = Rot([Buf(A(f"g_sq{i}", [128, TT], F32)) for i in range(2)])
    st = [Buf(A(f"g_st{i}", [128, TT], F32)) for i in range(4)]
    s1 = Buf(A("g_s1", [128, NW, NVB], F32))
    s2 = Buf(A("g_s2", [128, NW, NVB], F32))
    mv = Buf(A("g_mv", [128, 4, NW], F32))
    pool = Rot([Buf(PS(f"g_ps{i}", [128, 512], F32)) for i in range(6)])
    psS1 = Buf(PS("g_psS1", [128, TT], F32))
    psS2 = Buf(PS("g_psS2", [128, TT], F32))

    P.dma("sync", gb[:, :, :], io["gb"].rearrange("p (k c) -> p k c", k=2), writes=[gb])
    P.dma("sync", lng[:, :], io["lng"], writes=[lng])
    P.dma("sync", lnb[:, :], io["lnb"], writes=[lnb])
    P.dma("sync", wsf[:, :, :], io["wsT"].rearrange("p (g i) -> p g i", g=G), writes=[wsf])
    P.dma("sync", bsb[:, :, :], io["bsb"].rearrange("p (g i) -> p g i", g=G), writes=[bsb])
    P.op("vector", lambda e: e.memset(ones[:, :], 1.0), writes=[ones])
    P.op("vector", lambda e: e.memset(wsf[64:128, :, 0:64], 0.0), reads=[wsf], writes=[wsf])
    P.op("vector", lambda e: e.tensor_copy(out=wsb[:, :, :], in_=wsf[:, :, :]), reads=[wsf], writes=[wsb])
    for g in range(G):
        pr = pool.next()
        P.op("tensor", lambda e, g=g, pr=pr: e.matmul(pr[:, 0:128], lhsT=ones[:, :], rhs=wsf[:, g, :], start=True, stop=True),
             reads=[ones, wsf], writes=[pr])
        P.op("vector", lambda e, g=g, pr=pr: e.tensor_copy(out=rsb[:, g, :], in_=pr[:, 0:128]), reads=[pr], writes=[rsb])

    out_ops = []
    stages = []

    def run_stages():
        pend = None
        for i, (ld, comp) in enumerate(stages):
            cur = pend if pend is not None else ld()
            pend = stages[i + 1][0]() if i + 1 < len(stages) else None
            comp(cur)
        stages.clear()

    def ld_in(col0):
        def f():
            wb_ = wio.next()
            P.dma("gpsimd", wb_[:, :, :], w_in[:, col0:col0 + 512].rearrange("(c p) n -> p c n", p=128), writes=[wb_])
            return wb_
        return f

    for tt in range(NT):
        t0 = tt * TT
        P.dma("sync", xs[:, :, :], xT[:, t0:t0 + TT].rearrange("(c p) n -> p c n", p=128), writes=xs_res)
        for d in range(DC):
            P.op("vector", lambda e, d=d: e.tensor_copy(out=xb[:, d, :], in_=xs[:, d, :]), reads=[xs_res[d]], writes=[xb_res[d]])
        P.op("vector", lambda e: e.memset(s1[:, :, :], 0.0), writes=[s1])
        P.op("vector", lambda e: e.memset(s2[:, :, :], 0.0), writes=[s2])
        for vb in range(NVB):
            def comp_v(wb_, vb=vb):
                for w in range(NW):
                    ps = pool.next()
                    for kc in range(DC):
                        P.op("tensor", lambda e, ps=ps, kc=kc, w=w: e.matmul(
                            ps[:, :], lhsT=xb[:, kc, w * 128:(w + 1) * 128], rhs=wb_[:, kc, :],
                            start=(kc == 0), stop=(kc == DC - 1)), reads=[wb_, xb_res[kc]], writes=[ps])
                    vres = vt_res[vb * 4:(vb + 1) * 4]
                    P.op("scalar", lambda e, ps=ps, w=w: e.activation(
                        out=vt[:, vb * 4:(vb + 1) * 4, w, :], in_=ps[:, :].rearrange("p (a b) -> p a b", a=4),
                        func=AF.Gelu_apprx_tanh, accum_out=s1[:, w, vb:vb + 1]), reads=[ps], writes=vres + [s1])
                    jk = junk.next()
                    P.op("scalar", lambda e, jk=jk, w=w: e.activation(
                        out=jk[:, :].rearrange("p (a b) -> p a b", a=4), in_=vt[:, vb * 4:(vb + 1) * 4, w, :],
                        func=AF.Square, accum_out=s2[:, w, vb:vb + 1]), reads=vres, writes=[jk, s2])
            stages.append((ld_in(GH + vb * 512), comp_v))

        def comp_vnorm(wb_first_u):
            P.op("vector", lambda e: e.tensor_reduce(out=mv[:, 0, :], in_=s1[:, :, :], axis=AX.X, op=ALU.add), reads=[s1], writes=[mv])
            P.op("vector", lambda e: e.tensor_reduce(out=mv[:, 1, :], in_=s2[:, :, :], axis=AX.X, op=ALU.add), reads=[s2], writes=[mv])
            P.op("vector", lambda e: e.tensor_scalar(out=mv[:, 0:2, :], in0=mv[:, 0:2, :], scalar1=1.0 / GH, scalar2=None, op0=ALU.mult), reads=[mv], writes=[mv])
            P.op("vector", lambda e: e.tensor_tensor(out=mv[:, 2, :], in0=mv[:, 0, :], in1=mv[:, 0, :], op=ALU.mult), reads=[mv], writes=[mv])
            P.op("vector", lambda e: e.tensor_tensor(out=mv[:, 1, :], in0=mv[:, 1, :], in1=mv[:, 2, :], op=ALU.subtract), reads=[mv], writes=[mv])
            P.op("vector", lambda e: e.tensor_scalar(out=mv[:, 1, :], in0=mv[:, 1, :], scalar1=LN_EPS, scalar2=None, op0=ALU.add), reads=[mv], writes=[mv])
            P.op("scalar", lambda e: e.activation(out=mv[:, 1, :], in_=mv[:, 1, :], func=AF.Sqrt), reads=[mv], writes=[mv])
            P.op("vector", lambda e: e.reciprocal(out=mv[:, 2, :], in_=mv[:, 1, :]), reads=[mv], writes=[mv])
            P.op("vector", lambda e: e.scalar_tensor_tensor(out=mv[:, 3, :], in0=mv[:, 0, :], scalar=-1.0, in1=mv[:, 2, :],
                                                            op0=ALU.mult, op1=ALU.mult), reads=[mv], writes=[mv])
            for w in range(NW):
                eng = "vector" if w % 2 == 0 else "gpsimd"
                P.op(eng, lambda e, w=w: e.tensor_scalar(
                    out=vt[:, :, w, :], in0=vt[:, :, w, :], scalar1=mv[:, 2, w:w + 1], scalar2=mv[:, 3, w:w + 1],
                    op0=ALU.mult, op1=ALU.add), reads=[mv] + vt_res, writes=vt_res)

        for ub in range(NVB):
            def comp_u(wb_, ub=ub):
                if ub == 0:
                    comp_vnorm(None)
                for cc in range(4):
                    c = ub * 4 + cc
                    g = c // CPG
                    pu, psp = pool.next(), pool.next()
                    for kc in range(DC):
                        P.op("tensor", lambda e, pu=pu, kc=kc, cc=cc: e.matmul(
                            pu[:, :], lhsT=wb_[:, kc, cc * 128:(cc + 1) * 128], rhs=xb[:, kc, :],
                            start=(kc == 0), stop=(kc == DC - 1)), reads=[wb_, xb_res[kc]], writes=[pu])
                    for w in range(NW):
                        P.op("tensor", lambda e, psp=psp, w=w, c=c, g=g: e.matmul(
                            psp[:, w * 128:(w + 1) * 128], lhsT=vt[:, c, w, :], rhs=wsb[:, g, :], start=True, stop=True),
                            reads=[vt_res[c], wsb], writes=[psp])
                    u_ = ug.next()
                    s_ = sv.next()
                    a_ = ad.next()
                    P.op("scalar", lambda e, u_=u_, pu=pu: e.activation(out=u_[:, :], in_=pu[:, :], func=AF.Gelu_apprx_tanh),
                         reads=[pu], writes=[u_])
                    P.op("vector", lambda e, a_=a_, g=g, c=c: e.scalar_tensor_tensor(
                        out=a_[:, :], in0=rsb[:, g, :], scalar=gb[:, 1, c:c + 1], in1=bsb[:, g, :], op0=ALU.mult, op1=ALU.add),
                        reads=[rsb, gb, bsb], writes=[a_])
                    for w in range(NW):
                        P.op("vector", lambda e, s_=s_, psp=psp, a_=a_, c=c, w=w: e.scalar_tensor_tensor(
                            out=s_[:, w * 128:(w + 1) * 128], in0=psp[:, w * 128:(w + 1) * 128], scalar=gb[:, 0, c:c + 1],
                            in1=a_[:, :], op0=ALU.mult, op1=ALU.add), reads=[psp, a_, gb], writes=[s_])
                    P.op("vector", lambda e, s_=s_, u_=u_, c=c: e.tensor_tensor(
                        out=vt[:, c, :, :].rearrange("p w f -> p (w f)"), in0=u_[:, :], in1=s_[:, :], op=ALU.mult),
                        reads=[u_, s_], writes=[vt_res[c]])
            stages.append((ld_in(ub * 512), comp_u))

        for db in range(D // ndblk):
            def ld_o(db=db):
                wob = wo.next()
                P.dma("gpsimd", wob[:, :, :], w_out[:, db * ndblk:(db + 1) * ndblk].rearrange("(c p) n -> p c n", p=128), writes=[wob])
                return wob

            def comp_o(wob, db=db):
                for dc in range(ndblk // 128):
                    d = db * (ndblk // 128) + dc
                    po = pool.next()
                    for c in range(VC):
                        P.op("tensor", lambda e, po=po, c=c, dc=dc: e.matmul(
                            po[:, :], lhsT=wob[:, c, dc * 128:(dc + 1) * 128], rhs=vt[:, c, :, :].rearrange("p w f -> p (w f)"),
                            start=(c == 0), stop=(c == VC - 1)), reads=[wob, vt_res[c]], writes=[po])
                    P.op("vector", lambda e, po=po, d=d: e.scalar_tensor_tensor(
                        out=xs[:, d, :], in0=xs[:, d, :], scalar=float(alpha), in1=po[:, :], op0=ALU.mult, op1=ALU.add),
                        reads=[xs_res[d], po], writes=[xs_res[d]])
                    sq = sqbuf.next()
                    P.op("scalar", lambda e, sq=sq, d=d: e.activation(out=sq[:, :], in_=xs[:, d, :], func=AF.Square),
                         reads=[xs_res[d]], writes=[sq])
                    P.op("tensor", lambda e, d=d: e.matmul(psS1[:, :], lhsT=ones[:, :], rhs=xs[:, d, :],
                                                           start=(d == 0), stop=(d == DC - 1)),
                         reads=[ones, xs_res[d]], writes=[psS1])
                    P.op("tensor", lambda e, d=d, sq=sq: e.matmul(psS2[:, :], lhsT=ones[:, :], rhs=sq[:, :],
                                                                 start=(d == 0), stop=(d == DC - 1)),
                         reads=[ones, sq], writes=[psS2])
            stages.append((ld_o, comp_o))
        run_stages()
        ln_feature_major(P, nc, None, xs, xs_res, DC, TT, ones, psS1, psS2, lng, lnb, sqbuf, st, D)
        o = P.dma("sync", yT[:, t0:t0 + TT].rearrange("(c p) n -> p c n", p=128), xs[:, :, :], reads=xs_res)
        out_ops.append(o)
    return out_ops


RMS_EPS = 1e-6
NH = 16
SM_SCALE = 192 ** -0.5


def build_mla1(nc, P, io, D, T, TT=256):
    DC = D // 128
    NT = T // TT
    NW = TT // 128
    A = nc.alloc_sbuf_tensor
    PS = nc.alloc_psum_tensor
    xT = io["xT"]
    win = Buf(A("a_win", [128, DC, 1152], BF16))
    wq = Buf(A("a_wq", [128, 4, NH * 256], BF16))
    wkv = Buf(A("a_wkv", [128, 4, 4096], BF16))
    gq = Buf(A("a_gq", [128, 8], F32))
    ones = Buf(A("a_ones", [128, 128], F32))
    xb = Rot([Buf(A(f"a_xb{i}", [128, DC, TT], BF16)) for i in range(2)])
    rope = Rot([Buf(A(f"a_rope{i}", [64, 2, TT], F32)) for i in range(2)])
    cq = A("a_cq", [128, 8, TT], F32)
    cq_res = [Res() for _ in range(8)]
    cqn = A("a_cqn", [128, 8, TT], BF16)
    cqn_res = [Res() for _ in range(8)]
    sq = Rot([Buf(A(f"a_sq{i}", [128, TT], F32)) for i in range(2)])
    rstd = [Buf(A(f"a_rstd{i}", [128, TT], F32)) for i in range(2)]
    rtmp = Rot([Buf(A(f"a_rt{i}", [64, TT], F32)) for i in range(4)])
    big = Rot([Buf(A(f"a_big{i}", [128, NH, TT], BF16)) for i in range(3)])
    krb = Rot([Buf(A(f"a_kr{i}", [64, TT], BF16)) for i in range(2)])
    pool = Rot([Buf(PS(f"a_ps{i}", [128, 512], F32)) for i in range(6)])
    pss = [Buf(PS(f"a_pss{i}", [128, 512], F32)) for i in range(2)]

    for c0 in range(0, 1152, 384):
        P.dma("gpsimd", win[:, :, c0:c0 + 384], io["w_in"][:, c0:c0 + 384].rearrange("(c p) n -> p c n", p=128), writes=[win])
    for c0 in range(0, NH * 256, 1024):
        P.dma("gpsimd", wq[:, :, c0:c0 + 1024], io["w_q"][:, c0:c0 + 1024].rearrange("(c p) n -> p c n", p=128), writes=[wq])
    for c0 in range(0, 4096, 1024):
        P.dma("gpsimd", wkv[:, :, c0:c0 + 1024], io["w_kv"][:, c0:c0 + 1024].rearrange("(c p) n -> p c n", p=128), writes=[wkv])
    P.dma("sync", gq[:, :], io["gq"], writes=[gq])
    P.op("vector", lambda e: e.memset(ones[:, :], 1.0), writes=[ones])

    outs = []
    pend = None

    def load(tt):
        t0 = tt * TT
        xb_ = xb.next()
        rp = rope.next()
        P.dma("gpsimd", xb_[:, :, :], xT[:, t0:t0 + TT].rearrange("(c p) n -> p c n", p=128), writes=[xb_])
        P.dma("sync", rp[:, :, :], io["rope"].rearrange("p (k t) -> p k t", k=2)[:, :, t0:t0 + TT], writes=[rp])
        return xb_, rp

    def rope_combine(pr, prs, rp, out_ap, out_res):
        t1, t2 = rtmp.next(), rtmp.next()
        P.op("vector", lambda e: e.tensor_tensor(out=t1[:, :], in0=pr[0][0:64, pr[1]:pr[1] + TT], in1=rp[:, 0, :], op=ALU.mult),
             reads=[pr[0], rp], writes=[t1])
        P.op("vector", lambda e: e.tensor_tensor(out=t2[:, :], in0=prs[0][0:64, prs[1]:prs[1] + TT], in1=rp[:, 1, :], op=ALU.mult),
             reads=[prs[0], rp], writes=[t2])
        P.op("gpsimd", lambda e: e.tensor_tensor(out=out_ap, in0=t1[:, :], in1=t2[:, :], op=ALU.add),
             reads=[t1, t2], writes=out_res)

    def tile(tt, xb_, rp):
        t0 = tt * TT
        for fc in range(8):
            ps = pool.next()
            for kc in range(DC):
                P.op("tensor", lambda e, ps=ps, kc=kc, fc=fc: e.matmul(
                    ps[:, 0:TT], lhsT=win[:, kc, fc * 128:(fc + 1) * 128], rhs=xb_[:, kc, :],
                    start=(kc == 0), stop=(kc == DC - 1)), reads=[win, xb_], writes=[ps])
            P.op("scalar", lambda e, ps=ps, fc=fc: e.activation(out=cq[:, fc, :], in_=ps[:, 0:TT], func=AF.Copy),
                 reads=[ps], writes=[cq_res[fc]])
            s_ = sq.next()
            P.op("scalar", lambda e, ps=ps, s_=s_: e.activation(out=s_[:, :], in_=ps[:, 0:TT], func=AF.Square),
                 reads=[ps], writes=[s_])
            P.op("tensor", lambda e, s_=s_, fc=fc: e.matmul(pss[fc // 4][:, 0:TT], lhsT=ones[:, :], rhs=s_[:, :],
                                                            start=(fc % 4 == 0), stop=(fc % 4 == 3)),
                 reads=[ones, s_], writes=[pss[fc // 4]])
        prk = pool.next()
        for half in range(2):
            for kc in range(DC):
                P.op("tensor", lambda e, kc=kc, half=half: e.matmul(
                    prk[0:64, half * TT:(half + 1) * TT], lhsT=win[:, kc, 1024 + half * 64:1088 + half * 64], rhs=xb_[:, kc, :],
                    start=(kc == 0), stop=(kc == DC - 1)), reads=[win, xb_], writes=[prk])
        kr_ = krb.next()
        rope_combine((prk, 0), (prk, TT), rp, kr_[:, :], [kr_])
        outs.append(P.dma("sync", io["KR"][:, t0:t0 + TT], kr_[:, :], reads=[kr_]))
        for i in range(2):
            r_ = rstd[i]
            P.op("vector", lambda e, i=i, r_=r_: e.tensor_scalar(out=r_[:, :], in0=pss[i][:, 0:TT], scalar1=1.0 / 512, scalar2=RMS_EPS,
                                                                 op0=ALU.mult, op1=ALU.add), reads=[pss[i]], writes=[r_])
            P.op("scalar", lambda e, r_=r_: e.activation(out=r_[:, :], in_=r_[:, :], func=AF.Sqrt), reads=[r_], writes=[r_])
            P.op("vector", lambda e, r_=r_: e.reciprocal(out=r_[:, :], in_=r_[:, :]), reads=[r_], writes=[r_])
        for fc in range(8):
            P.op("vector", lambda e, fc=fc: e.scalar_tensor_tensor(
                out=cqn[:, fc, :], in0=cq[:, fc, :], scalar=gq[:, fc:fc + 1], in1=rstd[fc // 4][:, :], op0=ALU.mult, op1=ALU.mult),
                reads=[cq_res[fc], gq, rstd[fc // 4]], writes=[cqn_res[fc]])
        qn_, qr_ = big.next(), big.next()
        for h in range(NH):
            pq, pr = pool.next(), pool.next()
            for kc in range(4):
                P.op("tensor", lambda e, pq=pq, kc=kc, h=h: e.matmul(
                    pq[:, 0:TT], lhsT=wq[:, kc, h * 256:h * 256 + 128], rhs=cqn[:, kc, :],
                    start=(kc == 0), stop=(kc == 3)), reads=[wq, cqn_res[kc]], writes=[pq])
            for half in range(2):
                for kc in range(4):
                    P.op("tensor", lambda e, pr=pr, kc=kc, h=h, half=half: e.matmul(
                        pr[0:64, half * TT:(half + 1) * TT], lhsT=wq[:, kc, h * 256 + 128 + half * 64:h * 256 + 192 + half * 64],
                        rhs=cqn[:, kc, :], start=(kc == 0), stop=(kc == 3)), reads=[wq, cqn_res[kc]], writes=[pr])
            P.op("scalar", lambda e, pq=pq, h=h, qn_=qn_: e.activation(out=qn_[:, h, :], in_=pq[:, 0:TT], func=AF.Copy),
                 reads=[pq], writes=[qn_])
            rope_combine((pr, 0), (pr, TT), rp, qr_[0:64, h, :], [qr_])
        outs.append(P.dma("sync", io["Q"][:, :, t0:t0 + TT].rearrange("h p t -> p h t"), qn_[:, :, :], reads=[qn_]))
        outs.append(P.dma("sync", io["QR"][:, :, t0:t0 + TT].rearrange("h p t -> p h t"), qr_[0:64, :, :], reads=[qr_]))
        kn_ = big.next()
        for h in range(NH):
            pk = pool.next()
            for kc in range(4):
                P.op("tensor", lambda e, pk=pk, kc=kc, h=h: e.matmul(
                    pk[:, 0:TT], lhsT=wkv[:, kc, h * 128:(h + 1) * 128], rhs=cqn[:, 4 + kc, :],
                    start=(kc == 0), stop=(kc == 3)), reads=[wkv, cqn_res[4 + kc]], writes=[pk])
            eng = "scalar" if h % 2 == 0 else "vector"
            if eng == "scalar":
                P.op("scalar", lambda e, pk=pk, h=h, kn_=kn_: e.activation(out=kn_[:, h, :], in_=pk[:, 0:TT], func=AF.Copy),
                     reads=[pk], writes=[kn_])
            else:
                P.op("vector", lambda e, pk=pk, h=h, kn_=kn_: e.tensor_copy(out=kn_[:, h, :], in_=pk[:, 0:TT]),
                     reads=[pk], writes=[kn_])
        outs.append(P.dma("sync", io["K"][:, :, t0:t0 + TT].rearrange("h p t -> p h t"), kn_[:, :, :], reads=[kn_]))
        vt_ = big.next()
        vview = vt_.t[:, :, :].rearrange("p h t -> p (h t)").rearrange("p (w f) -> p w f", w=NW)
        for w in range(NW):
            for n in range(4):
                pv = pool.next()
                for kc in range(4):
                    P.op("tensor", lambda e, pv=pv, kc=kc, w=w, n=n: e.matmul(
                        pv[:, :], lhsT=cqn[:, 4 + kc, w * 128:(w + 1) * 128], rhs=wkv[:, kc, 2048 + n * 512:2048 + (n + 1) * 512],
                        start=(kc == 0), stop=(kc == 3)), reads=[wkv, cqn_res[4 + kc]], writes=[pv])
                if n % 2 == 0:
                    P.op("scalar", lambda e, pv=pv, w=w, n=n: e.activation(out=vview[:, w, n * 512:(n + 1) * 512], in_=pv[:, :], func=AF.Copy),
                         reads=[pv], writes=[vt_])
                else:
                    P.op("vector", lambda e, pv=pv, w=w, n=n: e.tensor_copy(out=vview[:, w, n * 512:(n + 1) * 512], in_=pv[:, :]),
                         reads=[pv], writes=[vt_])
        outs.append(P.dma("sync", io["V"][tt * NW:(tt + 1) * NW, :, :].rearrange("w p f -> p w f"), vview, reads=[vt_]))

    for tt in range(NT):
        cur = pend if pend is not None else load(tt)
        pend = load(tt + 1) if tt + 1 < NT else None
        tile(tt, cur[0], cur[1])
    return outs


def build_mla2(nc, P, io, D, T, TT=512, alpha=1.0):
    DC = D // 128
    NT = T // TT
    NB = T // 128
    A = nc.alloc_sbuf_tensor
    PS = nc.alloc_psum_tensor
    lng = Buf(A("b_lng", [128, DC], F32))
    lnb = Buf(A("b_lnb", [128, DC], F32))
    ones = Buf(A("b_ones", [128, 128], F32))
    onesb = Buf(A("b_onesb", [128, 128], BF16))
    pbias = Buf(A("b_pbias", [128, 1], F32))
    zbias = Buf(A("b_zbias", [128, 1], F32))
    kr = Buf(A("b_kr", [64, 2 * T], BF16))
    kh = Rot([Buf(A(f"b_kh{i}", [128, 2 * T], BF16)) for i in range(2)])
    vh = Rot([Buf(A(f"b_vh{i}", [128, 2 * NB, 128], BF16)) for i in range(2)])
    qh = Rot([Buf(A(f"b_qh{i}", [128, T], BF16)) for i in range(2)])
    qrh = Rot([Buf(A(f"b_qrh{i}", [64, T], BF16)) for i in range(2)])
    oT = A("b_oT", [128, NH, T], BF16)
    oT_res = [[Res() for _ in range(NT)] for _ in range(NH)]
    pT = Rot([Buf(A(f"b_pT{i}", [128, 512], BF16)) for i in range(4)])
    rs = Rot([Buf(A(f"b_rs{i}", [128, 512], F32)) for i in range(2)])
    xs = A("b_xs", [128, DC, TT], F32)
    xs_res = [Res() for _ in range(DC)]
    wo = Rot([Buf(A(f"b_wo{i}", [128, NH, 128], BF16)) for i in range(2)])
    sqbuf = Rot([Buf(A(f"b_sq{i}", [128, TT], F32)) for i in range(2)])
    st = [Buf(A(f"b_st{i}", [128, TT], F32)) for i in range(4)]
    spool = Rot([Buf(PS(f"b_sp{i}", [128, 512], F32)) for i in range(4)])
    apool = Rot([Buf(PS(f"b_ap{i}", [128, 512], F32)) for i in range(4)])

    P.dma("sync", lng[:, :], io["lng"], writes=[lng])
    P.dma("sync", lnb[:, :], io["lnb"], writes=[lnb])
    P.dma("sync", pbias[:, :], io["pbias"], writes=[pbias])
    P.op("vector", lambda e: e.memset(ones[:, :], 1.0), writes=[ones])
    P.op("vector", lambda e: e.memset(onesb[:, :], 1.0), writes=[onesb])
    P.op("vector", lambda e: e.memset(zbias[:, :], 0.0), writes=[zbias])
    P.dma("sync", kr[:, 0:T], io["KRp"], writes=[kr])
    P.dma("sync", kr[:, T:2 * T], io["KRo"], writes=[kr])

    def load_head(h):
        k_, v_, q_, qr_ = kh.next(), vh.next(), qh.next(), qrh.next()
        P.dma("sync", k_[:, 0:T], io["Kp"][h], writes=[k_])
        P.dma("sync", k_[:, T:2 * T], io["Ko"][h], writes=[k_])
        P.dma("sync", v_[:, 0:NB, :], io["Vp"][:, :, h * 128:(h + 1) * 128].rearrange("b p f -> p b f"), writes=[v_])
        P.dma("sync", v_[:, NB:2 * NB, :], io["Vo"][:, :, h * 128:(h + 1) * 128].rearrange("b p f -> p b f"), writes=[v_])
        P.dma("sync", q_[:, :], io["Q"][h], writes=[q_])
        P.dma("sync", qr_[:, :], io["QR"][h], writes=[qr_])
        return k_, v_, q_, qr_

    def qtile(h, qt, k_, v_, q_, qr_):
        if True:
            po, psm = apool.next(), apool.next()
            nkb = NB + 4 * qt + 4
            for kb in range(nkb):
                j = kb - (NB + 4 * qt)
                col0 = j * 128 if j > 0 else 0
                q0 = qt * 512 + col0
                q1 = (qt + 1) * 512
                ps = spool.next()
                P.op("tensor", lambda e, ps=ps, kb=kb, col0=col0, q0=q0, q1=q1: e.matmul(
                    ps[:, col0:512], lhsT=k_[:, kb * 128:(kb + 1) * 128], rhs=q_[:, q0:q1], start=True, stop=False),
                    reads=[k_, q_], writes=[ps])
                P.op("tensor", lambda e, ps=ps, kb=kb, col0=col0, q0=q0, q1=q1: e.matmul(
                    ps[:, col0:512], lhsT=kr[:, kb * 128:(kb + 1) * 128], rhs=qr_[:, q0:q1], start=False, stop=True),
                    reads=[kr, qr_], writes=[ps])
                p_ = pT.next()
                bias = pbias if kb < NB else zbias
                P.op("scalar", lambda e, ps=ps, p_=p_, col0=col0, bias=bias: e.activation(
                    out=p_[:, col0:512], in_=ps[:, col0:512], func=AF.Exp, bias=bias[:, 0:1], scale=SM_SCALE),
                    reads=[ps, bias], writes=[p_])
                if j >= 0:
                    P.op("vector", lambda e, p_=p_, col0=j * 128: e.memset(p_[64:128, col0:col0 + 64], 0.0),
                         reads=[p_], writes=[p_])
                P.op("tensor", lambda e, po=po, p_=p_, kb=kb, col0=col0: e.matmul(
                    po[:, col0:512], lhsT=v_[:, kb, :], rhs=p_[:, col0:512], start=(kb == 0), stop=(kb == nkb - 1)),
                    reads=[v_, p_], writes=[po])
                P.op("tensor", lambda e, psm=psm, p_=p_, kb=kb, col0=col0: e.matmul(
                    psm[:, col0:512], lhsT=onesb[:, :], rhs=p_[:, col0:512], start=(kb == 0), stop=(kb == nkb - 1)),
                    reads=[onesb, p_], writes=[psm])
            r_ = rs.next()
            P.op("vector", lambda e, r_=r_, psm=psm: e.reciprocal(out=r_[:, :], in_=psm[:, :]), reads=[psm], writes=[r_])
            P.op("vector", lambda e, r_=r_, po=po, h=h, qt=qt: e.tensor_tensor(
                out=oT[:, h, qt * 512:(qt + 1) * 512], in0=po[:, :], in1=r_[:, :], op=ALU.mult),
                reads=[po, r_], writes=[oT_res[h][qt]])


    pend = None
    for h in range(NH):
        cur = pend if pend is not None else load_head(h)
        pend = load_head(h + 1) if h + 1 < NH else None
        for qt in range(NT):
            qtile(h, qt, *cur)

    psS1, psS2 = spool.bufs[0], spool.bufs[1]
    opool = Rot(apool.bufs + spool.bufs[2:4])
    xT, w_out, yT = io["xT"], io["w_out"], io["yT"]
    outs = []
    pendw = None

    def ld_wo(d):
        wob = wo.next()
        P.dma("gpsimd", wob[:, :, :], w_out[:, d * 128:(d + 1) * 128].rearrange("(h p) n -> p h n", p=128), writes=[wob])
        return wob

    def p3tile(tt):
        nonlocal pendw
        t0 = tt * TT
        P.dma("sync", xs[:, :, :], xT[:, t0:t0 + TT].rearrange("(c p) n -> p c n", p=128), writes=xs_res)
        for d in range(DC):
            wob = pendw if pendw is not None else ld_wo(d)
            nxt = (tt * DC + d + 1)
            pendw = ld_wo(nxt % DC) if nxt < NT * DC else None
            po = opool.next()
            for h in range(NH):
                P.op("tensor", lambda e, po=po, h=h, wob=wob: e.matmul(
                    po[:, :], lhsT=wob[:, h, :], rhs=oT[:, h, t0:t0 + TT], start=(h == 0), stop=(h == NH - 1)),
                    reads=[wob, oT_res[h][tt]], writes=[po])
            P.op("vector", lambda e, po=po, d=d: e.scalar_tensor_tensor(
                out=xs[:, d, :], in0=xs[:, d, :], scalar=float(alpha), in1=po[:, :], op0=ALU.mult, op1=ALU.add),
                reads=[xs_res[d], po], writes=[xs_res[d]])
            sq = sqbuf.next()
            P.op("scalar", lambda e, sq=sq, d=d: e.activation(out=sq[:, :], in_=xs[:, d, :], func=AF.Square),
                 reads=[xs_res[d]], writes=[sq])
            P.op("tensor", lambda e, d=d: e.matmul(psS1[:, :], lhsT=ones[:, :], rhs=xs[:, d, :],
                                                   start=(d == 0), stop=(d == DC - 1)),
                 reads=[ones, xs_res[d]], writes=[psS1])
            P.op("tensor", lambda e, d=d, sq=sq: e.matmul(psS2[:, :], lhsT=ones[:, :], rhs=sq[:, :],
                                                         start=(d == 0), stop=(d == DC - 1)),
                 reads=[ones, sq], writes=[psS2])
        ln_feature_major(P, nc, None, xs, xs_res, DC, TT, ones, psS1, psS2, lng, lnb, sqbuf, st, D)
        outs.append(P.dma("sync", yT[:, t0:t0 + TT].rearrange("(c p) n -> p c n", p=128), xs[:, :, :], reads=xs_res))
    for tt in range(NT):
        p3tile(tt)
    return outs


D_MODEL = 2048
BATCH = 4
SEQ = 4096
DEPTH = 4
TCORE = 2048
NCORES = 8
GM_HALF = 6144
GM_GROUPS = 8
D_FF = 5504
ALPHA_DN = (2 * DEPTH) ** 0.25
DCH = D_MODEL // 128


def _mk(kind_inputs, outputs, builder):
    nc = bass.Bass("TRN2", target_bir_lowering=False)
    io = {}
    for n, shp, dt in kind_inputs:
        io[n] = nc.dram_tensor(n, list(shp), dt, kind="ExternalInput").ap()
    for n, shp, dt in outputs:
        io[n] = nc.dram_tensor(n, list(shp), dt, kind="ExternalOutput").ap()
    P = Prog(nc)
    outs = builder(nc, P, io)
    P.barrier_wait("sync", outs)
    P.emit()
    return nc


def _nc_ffn():
    NF = 2 * D_FF // 128
    return _mk([("xT", (D_MODEL, TCORE), F32), ("xh", (D_MODEL, 2), F32), ("w_up", (D_MODEL, 2 * D_FF), F32),
                ("cw", (128, 3 * NF), F32), ("cb", (128, NF), F32), ("w_down", (D_FF, D_MODEL), F32),
                ("lng", (128, DCH), F32), ("lnb", (128, DCH), F32)],
               [("yT", (D_MODEL, TCORE), F32)],
               lambda nc, P, io: build_ffn(nc, P, io, D_MODEL, D_FF, TCORE, alpha=ALPHA_DN, ncolblk=256, ndblk=256))


def _nc_gmlp():
    VC = GM_HALF // 128
    return _mk([("xT", (D_MODEL, TCORE), F32), ("w_in", (D_MODEL, 2 * GM_HALF), F32), ("gb", (128, 2 * VC), F32),
                ("wsT", (128, GM_GROUPS * 128), F32), ("bsb", (128, GM_GROUPS * 128), F32), ("w_out", (GM_HALF, D_MODEL), F32),
                ("lng", (128, DCH), F32), ("lnb", (128, DCH), F32)],
               [("yT", (D_MODEL, TCORE), F32)],
               lambda nc, P, io: build_gmlp(nc, P, io, D_MODEL, GM_HALF, GM_GROUPS, TCORE, alpha=ALPHA_DN))


def _nc_mla1():
    T = TCORE
    return _mk([("xT", (D_MODEL, T), F32), ("w_in", (D_MODEL, 1152), F32), ("gq", (128, 8), F32), ("w_q", (512, NH * 256), F32),
                ("w_kv", (512, 4096), F32), ("rope", (64, 2 * T), F32)],
               [("Q", (NH, 128, T), BF16), ("QR", (NH, 64, T), BF16), ("K", (NH, 128, T), BF16), ("KR", (64, T), BF16),
                ("V", (T // 128, 128, 2048), BF16)],
               lambda nc, P, io: build_mla1(nc, P, io, D_MODEL, T))


def _nc_mla2():
    T = TCORE
    ins = [("xT", (D_MODEL, T), F32), ("Q", (NH, 128, T), BF16), ("QR", (NH, 64, T), BF16)]
    for s in "po":
        ins += [("K" + s, (NH, 128, T), BF16), ("KR" + s, (64, T), BF16), ("V" + s, (T // 128, 128, 2048), BF16)]
    ins += [("pbias", (128, 1), F32), ("w_out", (2048, D_MODEL), F32), ("lng", (128, DCH), F32), ("lnb", (128, DCH), F32)]
    return _mk(ins, [("yT", (D_MODEL, T), F32)],
               lambda nc, P, io: build_mla2(nc, P, io, D_MODEL, T, alpha=ALPHA_DN))


def _pc(v, nchunk):
    return np.ascontiguousarray(np.asarray(v, np.float32).reshape(nchunk, 128).T)


def _run(nc, in_maps):
    res = run_bass_kernel_spmd(nc, in_maps, core_ids=list(range(NCORES)))
    return res.results


def kernel(x, gm_w_in, gm_ln_g, gm_ln_b, gm_w_s, gm_b_s, gm_w_out,
           mla_w_in, mla_q_norm_g, mla_kv_norm_g, mla_w_q_b, mla_w_kv_b, mla_w_out,
           ffn_w_up, ffn_conv_w, ffn_conv_b, ffn_w_down,
           ln_mix_g, ln_mix_b, ln_ffn_g, ln_ffn_b):
    x = np.asarray(x, np.float32)
    acts = []
    for c in range(NCORES):
        b, hf = divmod(c, 2)
        acts.append(np.ascontiguousarray(x[b, hf * TCORE:(hf + 1) * TCORE, :].T))
    half = 32
    inv_freq = (np.float32(10000.0) ** (-np.arange(half, dtype=np.float32) / np.float32(half))).astype(np.float32)
    ang = (np.arange(SEQ, dtype=np.float32)[:, None] * inv_freq[None, :]).astype(np.float32)
    cos, sin = np.cos(ang).astype(np.float32).T, np.sin(ang).astype(np.float32).T
    ropes = []
    for hf in range(2):
        c_, s_ = cos[:, hf * TCORE:(hf + 1) * TCORE], sin[:, hf * TCORE:(hf + 1) * TCORE]
        ropes.append(np.ascontiguousarray(np.concatenate([np.concatenate([c_, c_], 0), np.concatenate([-s_, s_], 0)], axis=1)))
    pb = [np.full((128, 1), -30000.0 if c % 2 == 0 else 0.0, np.float32) for c in range(NCORES)]
    NF = 2 * D_FF // 128
    VC = GM_HALF // 128
    for i in range(DEPTH):
        slot = i // 2
        if i % 2 == 0:
            com = {"w_in": np.ascontiguousarray(gm_w_in[slot], np.float32),
                   "gb": np.ascontiguousarray(np.concatenate([_pc(gm_ln_g[slot], VC), _pc(gm_ln_b[slot], VC)], axis=1)),
                   "wsT": np.ascontiguousarray(np.asarray(gm_w_s[slot], np.float32).transpose(2, 0, 1).reshape(128, GM_GROUPS * 128)),
                   "bsb": np.ascontiguousarray(np.broadcast_to(np.asarray(gm_b_s[slot], np.float32).reshape(1, -1), (128, GM_GROUPS * 128))),
                   "w_out": np.ascontiguousarray(gm_w_out[slot], np.float32),
                   "lng": _pc(ln_mix_g[i], DCH), "lnb": _pc(ln_mix_b[i], DCH)}
            res = _run(_nc_gmlp(), [dict(com, xT=acts[c]) for c in range(NCORES)])
            acts = [res[c]["yT"] for c in range(NCORES)]
        else:
            w_in = np.asarray(mla_w_in[slot], np.float32)
            w_in_ext = np.ascontiguousarray(np.concatenate([w_in, w_in[:, 1056:1088], w_in[:, 1024:1056]], axis=1))
            wq3 = np.asarray(mla_w_q_b[slot], np.float32).reshape(512, NH, 192)
            w_q_ext = np.ascontiguousarray(np.concatenate([wq3, wq3[:, :, 160:192], wq3[:, :, 128:160]], axis=2).reshape(512, NH * 256))
            wkv3 = np.asarray(mla_w_kv_b[slot], np.float32).reshape(512, NH, 256)
            w_kv_r = np.ascontiguousarray(np.concatenate([wkv3[:, :, :128].reshape(512, 2048), wkv3[:, :, 128:].reshape(512, 2048)], axis=1))
            gq = np.ascontiguousarray(np.concatenate([_pc(mla_q_norm_g[slot], 4), _pc(mla_kv_norm_g[slot], 4)], axis=1))
            com = {"w_in": w_in_ext, "gq": gq, "w_q": w_q_ext, "w_kv": w_kv_r}
            r1 = _run(_nc_mla1(), [dict(com, xT=acts[c], rope=ropes[c % 2]) for c in range(NCORES)])
            com2 = {"w_out": np.ascontiguousarray(mla_w_out[slot], np.float32), "lng": _pc(ln_mix_g[i], DCH), "lnb": _pc(ln_mix_b[i], DCH)}
            ims = []
            for c in range(NCORES):
                pc_ = c - (c % 2)
                ims.append(dict(com2, xT=acts[c], Q=r1[c]["Q"], QR=r1[c]["QR"], Kp=r1[pc_]["K"], KRp=r1[pc_]["KR"], Vp=r1[pc_]["V"],
                                Ko=r1[c]["K"], KRo=r1[c]["KR"], Vo=r1[c]["V"], pbias=pb[c]))
            res = _run(_nc_mla2(), ims)
            acts = [res[c]["yT"] for c in range(NCORES)]
        cwt = np.asarray(ffn_conv_w[i], np.float32)
        com = {"w_up": np.ascontiguousarray(ffn_w_up[i], np.float32),
               "cw": np.ascontiguousarray(cwt.reshape(3, NF, 128).transpose(2, 0, 1).reshape(128, 3 * NF)),
               "cb": _pc(ffn_conv_b[i], NF), "w_down": np.ascontiguousarray(ffn_w_down[i], np.float32),
               "lng": _pc(ln_ffn_g[i], DCH), "lnb": _pc(ln_ffn_b[i], DCH)}
        ims = []
        for c in range(NCORES):
            xh = np.ascontiguousarray(acts[c - 1][:, TCORE - 2:TCORE]) if c % 2 == 1 else np.zeros((D_MODEL, 2), np.float32)
            ims.append(dict(com, xT=acts[c], xh=xh))
        res = _run(_nc_ffn(), ims)
        acts = [res[c]["yT"] for c in range(NCORES)]
    out = np.empty((BATCH, SEQ, D_MODEL), np.float32)
    for c in range(NCORES):
        b, hf = divmod(c, 2)
        out[b, hf * TCORE:(hf + 1) * TCORE, :] = acts[c].T
    return out
```

```python
import numpy as np
import concourse.bass as bass
import concourse.mybir as mybir
from concourse.bass_utils import run_bass_kernel_spmd

F32 = mybir.dt.float32
BF16 = mybir.dt.bfloat16
AF = mybir.ActivationFunctionType
ALU = mybir.AluOpType
AX = mybir.AxisListType

PHYS = ["tensor", "vector", "scalar", "gpsimd", "sync"]


_UID = [0]


def uname(n):
    _UID[0] += 1
    return f"{n}_{_UID[0]}"


class Res:
    __slots__ = ("name", "w", "rs")

    def __init__(self, name=""):
        self.name = name
        self.w = None
        self.rs = []


class Op:
    __slots__ = ("eng", "fn", "deps", "semkey", "sem", "inc", "signal", "signo")


class Prog:
    def __init__(self, nc):
        self.nc = nc
        self.ops = {e: [] for e in PHYS}
        self.sems = {}
        self.phase = 0
        self.nops = 0
        self.last = {}

    def new_phase(self):
        self.phase += 1
        self.sems = {}

    def _sem(self, key):
        if key not in self.sems:
            self.sems[key] = self.nc.alloc_semaphore(name=f"s{self.phase}_{key}")
        return self.sems[key]

    def op(self, eng, fn, reads=(), writes=(), dma=False, semkey=None):
        o = Op()
        o.eng = eng
        o.fn = fn
        o.inc = 16 if dma else 1
        o.semkey = semkey or (eng + ("_dma" if dma else ""))
        o.sem = self._sem(o.semkey)
        o.signal = bool(dma)
        o.signo = 0
        deps = {}

        def add(d, kind):
            if d is None or d is o:
                return
            if d.eng == eng and d.inc == 1 and not dma:
                if eng == "tensor" or kind != "RAW":
                    return
            deps[id(d)] = d
            d.signal = True

        reads = [r.res if hasattr(r, "res") else r for r in reads]
        writes = [r.res if hasattr(r, "res") else r for r in writes]
        for r in reads:
            add(r.w, "RAW")
        for r in writes:
            add(r.w, "WAW")
            for rd in r.rs:
                add(rd, "WAR")
        for r in reads:
            r.rs.append(o)
        for r in writes:
            r.w = o
            r.rs = []
        o.deps = list(deps.values())
        self.ops[eng].append(o)
        self.last[(eng, o.semkey)] = o
        self.nops += 1
        return o

    def dma(self, eng, out, in_, reads=(), writes=(), semkey=None):
        return self.op(eng, lambda e: e.dma_start(out=out, in_=in_), reads, writes,
                       dma=True, semkey=semkey)

    def barrier_wait(self, eng, ops):
        o = Op()
        o.eng = eng
        o.fn = None
        o.inc = 1
        o.semkey = eng
        o.sem = None
        o.signal = False
        o.signo = 0
        for d in ops:
            d.signal = True
        o.deps = list(ops)
        self.ops[eng].append(o)
        return o

    def full_barrier(self, renew=False):
        lasts = list(self.last.values())
        for e in PHYS:
            self.barrier_wait(e, lasts)
        if renew:
            self.last = {}
            self.new_phase()

    def emit(self):
        cnt = {}
        for e in PHYS:
            for o in self.ops[e]:
                if o.signal:
                    k = id(o.sem)
                    cnt[k] = cnt.get(k, 0) + o.inc
                    o.signo = cnt[k]
        self.maxsem = max(cnt.values()) if cnt else 0
        nc = self.nc
        nwaits = [0]
        with nc.Block() as block:
            for e in PHYS:
                if not self.ops[e]:
                    continue

                def body(engobj, e=e):
                    known = {}
                    for o in self.ops[e]:
                        need = {}
                        for d in o.deps:
                            k = id(d.sem)
                            if k not in need or need[k][1] < d.signo:
                                need[k] = (d.sem, d.signo)
                        for k, (sem, v) in need.items():
                            if known.get(k, 0) < v:
                                engobj.wait_ge(sem, v)
                                known[k] = v
                                nwaits[0] += 1
                        if o.fn is not None:
                            ins = o.fn(engobj)
                            if o.signal:
                                ins.then_inc(o.sem, o.inc)

                getattr(block, e)(body)
        self.nwaits = nwaits[0]


class Buf:
    def __init__(self, t, name=""):
        self.t = t
        self.res = Res(name)
        self.w = None

    def __getitem__(self, k):
        return self.t[k]


LN_EPS = 1e-5


def cdiv(a, b):
    return (a + b - 1) // b


class Rot:
    def __init__(self, bufs):
        self.bufs = bufs
        self.i = 0

    def next(self):
        b = self.bufs[self.i % len(self.bufs)]
        self.i += 1
        return b


def ln_feature_major(P, nc, C, r, rres, DC, TT, ones, psS1, psS2, lng, lnb, sq_rot, st, D):
    mean, msq, var, rstd = st
    P.op("vector", lambda e: e.tensor_scalar(out=mean[:, :], in0=psS1[:, :], scalar1=1.0 / D, scalar2=None, op0=ALU.mult),
         reads=[psS1], writes=[mean])
    P.op("vector", lambda e: e.tensor_tensor(out=msq[:, :], in0=mean[:, :], in1=mean[:, :], op=ALU.mult),
         reads=[mean], writes=[msq])
    P.op("vector", lambda e: e.scalar_tensor_tensor(out=var[:, :], in0=psS2[:, :], scalar=1.0 / D, in1=msq[:, :],
                                                    op0=ALU.mult, op1=ALU.subtract),
         reads=[psS2, msq], writes=[var])
    P.op("vector", lambda e: e.tensor_scalar(out=var[:, :], in0=var[:, :], scalar1=LN_EPS, scalar2=None, op0=ALU.add),
         reads=[var], writes=[var])
    P.op("scalar", lambda e: e.activation(out=msq[:, :], in_=var[:, :], func=AF.Sqrt),
         reads=[var], writes=[msq])
    P.op("vector", lambda e: e.reciprocal(out=rstd[:, :], in_=msq[:, :]), reads=[msq], writes=[rstd])
    for d in range(DC):
        P.op("vector", lambda e, d=d: e.tensor_tensor(out=r[:, d, :], in0=r[:, d, :], in1=mean[:, :], op=ALU.subtract),
             reads=[rres[d], mean], writes=[rres[d]])
        P.op("gpsimd", lambda e, d=d: e.tensor_tensor(out=r[:, d, :], in0=r[:, d, :], in1=rstd[:, :], op=ALU.mult),
             reads=[rres[d], rstd], writes=[rres[d]])
        P.op("scalar", lambda e, d=d: e.activation(out=r[:, d, :], in_=r[:, d, :], func=AF.Identity,
                                                   bias=lnb[:, d:d + 1], scale=lng[:, d:d + 1]),
             reads=[rres[d]], writes=[rres[d]])


def build_ffn(nc, P, io, D, DFF, T, TT=512, alpha=1.0, ncolblk=512, ndblk=256):
    DC = D // 128
    HC = DFF // 128
    NF = 2 * HC
    NT = T // TT
    xT, xh, w_up, w_down, yT = io["xT"], io["xh"], io["w_up"], io["w_down"], io["yT"]

    A = lambda name, shape, dt: nc.alloc_sbuf_tensor(uname(name), shape, dt)
    cw = Buf(A("f_cw", [128, 3, NF], F32))
    cb = Buf(A("f_cb", [128, NF], F32))
    lng = Buf(A("f_lng", [128, DC], F32))
    lnb = Buf(A("f_lnb", [128, DC], F32))
    ones = Buf(A("f_ones", [128, 128], F32))
    xs = Buf(A("f_xs", [128, DC, TT], F32))
    xs_res = [Res() for _ in range(DC)]
    xb = A("f_xb", [128, DC, TT], BF16)
    xb_res = [Res() for _ in range(DC)]
    xhs = Buf(A("f_xhs", [128, DC, 2], F32))
    xhb = Buf(A("f_xhb", [128, DC, 2], BF16))
    hid = A("f_hid", [128, HC, TT], BF16)
    hid_res = [Res() for _ in range(HC)]
    halo = A("f_halo", [128, NF, 2], F32)
    halo_res = [Res() for _ in range(NF)]
    wa = Rot([Buf(A(f"f_wa{i}", [128, DC, ncolblk], BF16)) for i in range(2)])
    wg = Rot([Buf(A(f"f_wg{i}", [128, DC, ncolblk], BF16)) for i in range(2)])
    wd = Rot([Buf(A(f"f_wd{i}", [128, HC, ndblk], BF16)) for i in range(2)])
    hbuf = Rot([Buf(A(f"f_hbuf{i}", [128, TT + 2], F32)) for i in range(4)])
    ybuf = Rot([Buf(A(f"f_y{i}", [128, TT], F32)) for i in range(4)])
    sgbuf = Rot([Buf(A(f"f_sg{i}", [128, TT], F32)) for i in range(2)])
    sqbuf = Rot([Buf(A(f"f_sq{i}", [128, TT], F32)) for i in range(2)])
    st = [Buf(A(f"f_st{i}", [128, TT], F32)) for i in range(4)]
    PS = lambda name, shape, dt: nc.alloc_psum_tensor(uname(name), shape, dt)
    pool = Rot([Buf(PS(f"f_ps{i}", [128, TT], F32)) for i in range(5)])
    psH = PS("f_psH", [128, 512], F32)
    psH_res = [Res() for _ in range(HC)]
    psS1 = Buf(PS("f_psS1", [128, TT], F32))
    psS2 = Buf(PS("f_psS2", [128, TT], F32))

    P.dma("sync", cw[:, :, :], io["cw"].rearrange("p (k c) -> p k c", k=3), writes=[cw])
    P.dma("sync", cb[:, :], io["cb"], writes=[cb])
    P.dma("sync", lng[:, :], io["lng"], writes=[lng])
    P.dma("sync", lnb[:, :], io["lnb"], writes=[lnb])
    P.op("vector", lambda e: e.memset(ones[:, :], 1.0), writes=[ones])
    P.dma("sync", xhs[:, :, :], xh.rearrange("(c p) n -> p c n", p=128), writes=[xhs])
    flag = Buf(A("f_flag", [128, 1], F32))
    P.dma("sync", flag[:, :], io["flag"], writes=[flag])
    P.op("vector", lambda e: e.tensor_scalar(out=xhb[:, :, :], in0=xhs[:, :, :], scalar1=flag[:, 0:1], scalar2=None, op0=ALU.mult),
         reads=[xhs, flag], writes=[xhb])

    out_ops = []
    stages = []

    def run_stages():
        pend = None
        for i, (ld, comp) in enumerate(stages):
            cur = pend if pend is not None else ld()
            pend = stages[i + 1][0]() if i + 1 < len(stages) else None
            comp(cur)
        stages.clear()

    for tt in range(NT):
        t0 = tt * TT
        P.dma("sync", xs[:, :, :], xT[:, t0:t0 + TT].rearrange("(c p) n -> p c n", p=128), writes=xs_res)
        for d in range(DC):
            eng = "vector" if d % 2 == 0 else "gpsimd"
            P.op(eng, lambda e, d=d: e.tensor_copy(out=xb[:, d, :], in_=xs[:, d, :]), reads=[xs_res[d]], writes=[xb_res[d]])
        nblk = cdiv(HC, ncolblk // 128)
        for blk in range(nblk):
            c0 = blk * (ncolblk // 128)
            nch = min(ncolblk // 128, HC - c0)

            def ld_up(c0=c0, nch=nch):
                wab, wgb = wa.next(), wg.next()
                P.dma("gpsimd", wab[:, :, :nch * 128],
                      w_up[:, c0 * 128:(c0 + nch) * 128].rearrange("(c p) n -> p c n", p=128), writes=[wab])
                P.dma("gpsimd", wgb[:, :, :nch * 128],
                      w_up[:, DFF + c0 * 128:DFF + (c0 + nch) * 128].rearrange("(c p) n -> p c n", p=128), writes=[wgb])
                return wab, wgb

            def comp_up(bufs, c0=c0, nch=nch, tt=tt):
                wab, wgb = bufs
                for cc in range(nch):
                    c = c0 + cc
                    pa, pg = pool.next(), pool.next()
                    for (pp, wb_) in ((pa, wab), (pg, wgb)):
                        for kc in range(DC):
                            P.op("tensor", lambda e, pp=pp, wb_=wb_, kc=kc, cc=cc: e.matmul(
                                pp[:, :], lhsT=wb_[:, kc, cc * 128:(cc + 1) * 128], rhs=xb[:, kc, :],
                                start=(kc == 0), stop=(kc == DC - 1)), reads=[wb_, xb_res[kc]], writes=[pp])
                    if tt == 0:
                        for hi, wb_ in enumerate((wab, wgb)):
                            for kc in range(DC):
                                P.op("tensor", lambda e, hi=hi, wb_=wb_, kc=kc, cc=cc, c=c: e.matmul(
                                    psH[:, c * 4 + hi * 2:c * 4 + hi * 2 + 2], lhsT=wb_[:, kc, cc * 128:(cc + 1) * 128],
                                    rhs=xhb[:, kc, :], start=(kc == 0), stop=(kc == DC - 1)),
                                    reads=[wb_, xhb], writes=[psH_res[c]])
                    ys = []
                    for hi, pp in enumerate((pa, pg)):
                        fc = c + hi * HC
                        hb = hbuf.next()
                        y = ybuf.next()
                        P.op("scalar", lambda e, hb=hb, pp=pp: e.activation(out=hb[:, 2:TT + 2], in_=pp[:, :], func=AF.Copy),
                             reads=[pp], writes=[hb])
                        if tt == 0:
                            P.op("vector", lambda e, hb=hb, c=c, hi=hi: e.tensor_copy(
                                out=hb[:, 0:2], in_=psH[:, c * 4 + hi * 2:c * 4 + hi * 2 + 2]),
                                reads=[psH_res[c]], writes=[hb])
                        else:
                            P.op("vector", lambda e, hb=hb, fc=fc: e.tensor_copy(out=hb[:, 0:2], in_=halo[:, fc, :]),
                                 reads=[halo_res[fc]], writes=[hb])
                        P.op("vector", lambda e, hb=hb, fc=fc: e.tensor_copy(out=halo[:, fc, :], in_=hb[:, TT:TT + 2]),
                             reads=[hb], writes=[halo_res[fc]])
                        P.op("scalar", lambda e, y=y, pp=pp, fc=fc: e.activation(
                            out=y[:, :], in_=pp[:, :], func=AF.Identity, bias=cb[:, fc:fc + 1], scale=cw[:, 2, fc:fc + 1]),
                            reads=[pp, cb, cw], writes=[y])
                        P.op("vector", lambda e, y=y, hb=hb, fc=fc: e.scalar_tensor_tensor(
                            out=y[:, :], in0=hb[:, 1:TT + 1], scalar=cw[:, 1, fc:fc + 1], in1=y[:, :],
                            op0=ALU.mult, op1=ALU.add), reads=[hb, y, cw], writes=[y])
                        P.op("vector", lambda e, y=y, hb=hb, fc=fc: e.scalar_tensor_tensor(
                            out=y[:, :], in0=hb[:, 0:TT], scalar=cw[:, 0, fc:fc + 1], in1=y[:, :],
                            op0=ALU.mult, op1=ALU.add), reads=[hb, y, cw], writes=[y])
                        ys.append(y)
                    ya, yg = ys
                    sg = sgbuf.next()
                    P.op("scalar", lambda e, sg=sg, yg=yg: e.activation(out=sg[:, :], in_=yg[:, :], func=AF.Silu),
                         reads=[yg], writes=[sg])
                    P.op("vector", lambda e, sg=sg, ya=ya, c=c: e.tensor_tensor(
                        out=hid[:, c, :], in0=sg[:, :], in1=ya[:, :], op=ALU.mult), reads=[sg, ya], writes=[hid_res[c]])

            stages.append((ld_up, comp_up))
        ndb = D // ndblk
        for db in range(ndb):
            def ld_dn(db=db):
                wdb = wd.next()
                P.dma("gpsimd", wdb[:, :, :], w_down[:, db * ndblk:(db + 1) * ndblk].rearrange("(c p) n -> p c n", p=128),
                      writes=[wdb])
                return wdb

            def comp_dn(wdb, db=db):
                for dc in range(ndblk // 128):
                    d = db * (ndblk // 128) + dc
                    po = pool.next()
                    for c in range(HC):
                        P.op("tensor", lambda e, po=po, wdb=wdb, c=c, dc=dc: e.matmul(
                            po[:, :], lhsT=wdb[:, c, dc * 128:(dc + 1) * 128], rhs=hid[:, c, :],
                            start=(c == 0), stop=(c == HC - 1)), reads=[wdb, hid_res[c]], writes=[po])
                    P.op("vector", lambda e, po=po, d=d: e.scalar_tensor_tensor(
                        out=xs[:, d, :], in0=xs[:, d, :], scalar=float(alpha), in1=po[:, :], op0=ALU.mult, op1=ALU.add),
                        reads=[xs_res[d], po], writes=[xs_res[d]])
                    sq = sqbuf.next()
                    P.op("scalar", lambda e, sq=sq, d=d: e.activation(out=sq[:, :], in_=xs[:, d, :], func=AF.Square),
                         reads=[xs_res[d]], writes=[sq])
                    extra = psH_res if d == 0 else []
                    P.op("tensor", lambda e, d=d: e.matmul(psS1[:, :], lhsT=ones[:, :], rhs=xs[:, d, :],
                                                           start=(d == 0), stop=(d == DC - 1)),
                         reads=[ones, xs_res[d]], writes=[psS1] + extra)
                    P.op("tensor", lambda e, d=d, sq=sq: e.matmul(psS2[:, :], lhsT=ones[:, :], rhs=sq[:, :],
                                                                 start=(d == 0), stop=(d == DC - 1)),
                         reads=[ones, sq], writes=[psS2])

            stages.append((ld_dn, comp_dn))
        run_stages()
        ln_feature_major(P, nc, None, xs, xs_res, DC, TT, ones, psS1, psS2, lng, lnb, sqbuf, st, D)
        o = P.dma("sync", yT[:, t0:t0 + TT].rearrange("(c p) n -> p c n", p=128), xs[:, :, :], reads=xs_res)
        out_ops.append(o)
    return out_ops


def build_gmlp(nc, P, io, D, GH, G, T, TT=512, alpha=1.0, ndblk=128):
    DC = D // 128
    VC = GH // 128
    CPG = VC // G
    NT = T // TT
    NW = TT // 128
    NVB = GH // 512
    xT, w_in, w_out, yT = io["xT"], io["w_in"], io["w_out"], io["yT"]
    A = lambda name, shape, dt: nc.alloc_sbuf_tensor(uname(name), shape, dt)
    PS = lambda name, shape, dt: nc.alloc_psum_tensor(uname(name), shape, dt)

    gb = Buf(A("g_gb", [128, 2, VC], F32))
    lng = Buf(A("g_lng", [128, DC], F32))
    lnb = Buf(A("g_lnb", [128, DC], F32))
    ones = Buf(A("g_ones", [128, 128], F32))
    wsf = Buf(A("g_wsf", [128, G, 128], F32))
    wsb = Buf(A("g_wsb", [128, G, 128], BF16))
    rsb = Buf(A("g_rsb", [128, G, 128], F32))
    bsb = Buf(A("g_bsb", [128, G, 128], F32))
    xs = A("g_xs", [128, DC, TT], F32)
    xs_res = [Res() for _ in range(DC)]
    xb = A("g_xb", [128, DC, TT], BF16)
    xb_res = [Res() for _ in range(DC)]
    vt = A("g_vt", [128, VC, NW, 128], BF16)
    vt_res = [Res() for _ in range(VC)]
    wio = Rot([Buf(A(f"g_wio{i}", [128, DC, 512], BF16)) for i in range(2)])
    wo = Rot([Buf(A(f"g_wo{i}", [128, VC, ndblk], BF16)) for i in range(2)])
    ug = Rot([Buf(A(f"g_ug{i}", [128, TT], F32)) for i in range(2)])
    sv = Rot([Buf(A(f"g_sv{i}", [128, TT], F32)) for i in range(2)])
    ad = Rot([Buf(A(f"g_ad{i}", [128, 128], F32)) for i in range(2)])
    junk = Rot([Buf(A(f"g_junk{i}", [128, 512], BF16)) for i in range(2)])
    sqbuf = Rot([Buf(A(f"g_sq{i}", [128, TT], F32)) for i in range(2)])
    st = [Buf(A(f"g_st{i}", [128, TT], F32)) for i in range(4)]
    s1 = Buf(A("g_s1", [128, NW, NVB], F32))
    s2 = Buf(A("g_s2", [128, NW, NVB], F32))
    mv = Buf(A("g_mv", [128, 4, NW], F32))
    pool = Rot([Buf(PS(f"g_ps{i}", [128, 512], F32)) for i in range(6)])
    psS1 = Buf(PS("g_psS1", [128, TT], F32))
    psS2 = Buf(PS("g_psS2", [128, TT], F32))

    P.dma("sync", gb[:, :, :], io["gb"].rearrange("p (k c) -> p k c", k=2), writes=[gb])
    P.dma("sync", lng[:, :], io["lng"], writes=[lng])
    P.dma("sync", lnb[:, :], io["lnb"], writes=[lnb])
    P.dma("sync", wsf[:, :, :], io["wsT"].rearrange("p (g i) -> p g i", g=G), writes=[wsf])
    P.dma("sync", bsb[:, :, :], io["bsb"].rearrange("p (g i) -> p g i", g=G), writes=[bsb])
    P.op("vector", lambda e: e.memset(ones[:, :], 1.0), writes=[ones])
    P.op("vector", lambda e: e.memset(wsf[64:128, :, 0:64], 0.0), reads=[wsf], writes=[wsf])
    P.op("vector", lambda e: e.tensor_copy(out=wsb[:, :, :], in_=wsf[:, :, :]), reads=[wsf], writes=[wsb])
    for g in range(G):
        pr = pool.next()
        P.op("tensor", lambda e, g=g, pr=pr: e.matmul(pr[:, 0:128], lhsT=ones[:, :], rhs=wsf[:, g, :], start=True, stop=True),
             reads=[ones, wsf], writes=[pr])
        P.op("vector", lambda e, g=g, pr=pr: e.tensor_copy(out=rsb[:, g, :], in_=pr[:, 0:128]), reads=[pr], writes=[rsb])

    out_ops = []
    stages = []

    def run_stages():
        pend = None
        for i, (ld, comp) in enumerate(stages):
            cur = pend if pend is not None else ld()
            pend = stages[i + 1][0]() if i + 1 < len(stages) else None
            comp(cur)
        stages.clear()

    def ld_in(col0):
        def f():
            wb_ = wio.next()
            P.dma("gpsimd", wb_[:, :, :], w_in[:, col0:col0 + 512].rearrange("(c p) n -> p c n", p=128), writes=[wb_])
            return wb_
        return f

    for tt in range(NT):
        t0 = tt * TT
        P.dma("sync", xs[:, :, :], xT[:, t0:t0 + TT].rearrange("(c p) n -> p c n", p=128), writes=xs_res)
        for d in range(DC):
            P.op("vector", lambda e, d=d: e.tensor_copy(out=xb[:, d, :], in_=xs[:, d, :]), reads=[xs_res[d]], writes=[xb_res[d]])
        P.op("vector", lambda e: e.memset(s1[:, :, :], 0.0), writes=[s1])
        P.op("vector", lambda e: e.memset(s2[:, :, :], 0.0), writes=[s2])
        for vb in range(NVB):
            def comp_v(wb_, vb=vb):
                for w in range(NW):
                    ps = pool.next()
                    for kc in range(DC):
                        P.op("tensor", lambda e, ps=ps, kc=kc, w=w: e.matmul(
                            ps[:, :], lhsT=xb[:, kc, w * 128:(w + 1) * 128], rhs=wb_[:, kc, :],
                            start=(kc == 0), stop=(kc == DC - 1)), reads=[wb_, xb_res[kc]], writes=[ps])
                    vres = vt_res[vb * 4:(vb + 1) * 4]
                    P.op("scalar", lambda e, ps=ps, w=w: e.activation(
                        out=vt[:, vb * 4:(vb + 1) * 4, w, :], in_=ps[:, :].rearrange("p (a b) -> p a b", a=4),
                        func=AF.Gelu_apprx_tanh, accum_out=s1[:, w, vb:vb + 1]), reads=[ps], writes=vres + [s1])
                    jk = junk.next()
                    P.op("scalar", lambda e, jk=jk, w=w: e.activation(
                        out=jk[:, :].rearrange("p (a b) -> p a b", a=4), in_=vt[:, vb * 4:(vb + 1) * 4, w, :],
                        func=AF.Square, accum_out=s2[:, w, vb:vb + 1]), reads=vres, writes=[jk, s2])
            stages.append((ld_in(GH + vb * 512), comp_v))

        def comp_vnorm(wb_first_u):
            P.op("vector", lambda e: e.tensor_reduce(out=mv[:, 0, :], in_=s1[:, :, :], axis=AX.X, op=ALU.add), reads=[s1], writes=[mv])
            P.op("vector", lambda e: e.tensor_reduce(out=mv[:, 1, :], in_=s2[:, :, :], axis=AX.X, op=ALU.add), reads=[s2], writes=[mv])
            P.op("vector", lambda e: e.tensor_scalar(out=mv[:, 0:2, :], in0=mv[:, 0:2, :], scalar1=1.0 / GH, scalar2=None, op0=ALU.mult), reads=[mv], writes=[mv])
            P.op("vector", lambda e: e.tensor_tensor(out=mv[:, 2, :], in0=mv[:, 0, :], in1=mv[:, 0, :], op=ALU.mult), reads=[mv], writes=[mv])
            P.op("vector", lambda e: e.tensor_tensor(out=mv[:, 1, :], in0=mv[:, 1, :], in1=mv[:, 2, :], op=ALU.subtract), reads=[mv], writes=[mv])
            P.op("vector", lambda e: e.tensor_scalar(out=mv[:, 1, :], in0=mv[:, 1, :], scalar1=LN_EPS, scalar2=None, op0=ALU.add), reads=[mv], writes=[mv])
            P.op("scalar", lambda e: e.activation(out=mv[:, 1, :], in_=mv[:, 1, :], func=AF.Sqrt), reads=[mv], writes=[mv])
            P.op("vector", lambda e: e.reciprocal(out=mv[:, 2, :], in_=mv[:, 1, :]), reads=[mv], writes=[mv])
            P.op("vector", lambda e: e.scalar_tensor_tensor(out=mv[:, 3, :], in0=mv[:, 0, :], scalar=-1.0, in1=mv[:, 2, :],
                                                            op0=ALU.mult, op1=ALU.mult), reads=[mv], writes=[mv])
            for w in range(NW):
                eng = "vector" if w % 2 == 0 else "gpsimd"
                P.op(eng, lambda e, w=w: e.tensor_scalar(
                    out=vt[:, :, w, :], in0=vt[:, :, w, :], scalar1=mv[:, 2, w:w + 1], scalar2=mv[:, 3, w:w + 1],
                    op0=ALU.mult, op1=ALU.add), reads=[mv] + vt_res, writes=vt_res)

        for ub in range(NVB):
            def comp_u(wb_, ub=ub):
                if ub == 0:
                    comp_vnorm(None)
                for cc in range(4):
                    c = ub * 4 + cc
                    g = c // CPG
                    pu, psp = pool.next(), pool.next()
                    for kc in range(DC):
                        P.op("tensor", lambda e, pu=pu, kc=kc, cc=cc: e.matmul(
                            pu[:, :], lhsT=wb_[:, kc, cc * 128:(cc + 1) * 128], rhs=xb[:, kc, :],
                            start=(kc == 0), stop=(kc == DC - 1)), reads=[wb_, xb_res[kc]], writes=[pu])
                    for w in range(NW):
                        P.op("tensor", lambda e, psp=psp, w=w, c=c, g=g: e.matmul(
                            psp[:, w * 128:(w + 1) * 128], lhsT=vt[:, c, w, :], rhs=wsb[:, g, :], start=True, stop=True),
                            reads=[vt_res[c], wsb], writes=[psp])
                    u_ = ug.next()
                    s_ = sv.next()
                    a_ = ad.next()
                    P.op("scalar", lambda e, u_=u_, pu=pu: e.activation(out=u_[:, :], in_=pu[:, :], func=AF.Gelu_apprx_tanh),
                         reads=[pu], writes=[u_])
                    P.op("vector", lambda e, a_=a_, g=g, c=c: e.scalar_tensor_tensor(
                        out=a_[:, :], in0=rsb[:, g, :], scalar=gb[:, 1, c:c + 1], in1=bsb[:, g, :], op0=ALU.mult, op1=ALU.add),
                        reads=[rsb, gb, bsb], writes=[a_])
                    for w in range(NW):
                        P.op("vector", lambda e, s_=s_, psp=psp, a_=a_, c=c, w=w: e.scalar_tensor_tensor(
                            out=s_[:, w * 128:(w + 1) * 128], in0=psp[:, w * 128:(w + 1) * 128], scalar=gb[:, 0, c:c + 1],
                            in1=a_[:, :], op0=ALU.mult, op1=ALU.add), reads=[psp, a_, gb], writes=[s_])
                    P.op("vector", lambda e, s_=s_, u_=u_, c=c: e.tensor_tensor(
                        out=vt[:, c, :, :].rearrange("p w f -> p (w f)"), in0=u_[:, :], in1=s_[:, :], op=ALU.mult),
                        reads=[u_, s_], writes=[vt_res[c]])
            stages.append((ld_in(ub * 512), comp_u))

        for db in range(D // ndblk):
            def ld_o(db=db):
                wob = wo.next()
                P.dma("gpsimd", wob[:, :, :], w_out[:, db * ndblk:(db + 1) * ndblk].rearrange("(c p) n -> p c n", p=128), writes=[wob])
                return wob

            def comp_o(wob, db=db):
                for dc in range(ndblk // 128):
                    d = db * (ndblk // 128) + dc
                    po = pool.next()
                    for c in range(VC):
                        P.op("tensor", lambda e, po=po, c=c, dc=dc: e.matmul(
                            po[:, :], lhsT=wob[:, c, dc * 128:(dc + 1) * 128], rhs=vt[:, c, :, :].rearrange("p w f -> p (w f)"),
                            start=(c == 0), stop=(c == VC - 1)), reads=[wob, vt_res[c]], writes=[po])
                    P.op("vector", lambda e, po=po, d=d: e.scalar_tensor_tensor(
                        out=xs[:, d, :], in0=xs[:, d, :], scalar=float(alpha), in1=po[:, :], op0=ALU.mult, op1=ALU.add),
                        reads=[xs_res[d], po], writes=[xs_res[d]])
                    sq = sqbuf.next()
                    P.op("scalar", lambda e, sq=sq, d=d: e.activation(out=sq[:, :], in_=xs[:, d, :], func=AF.Square),
                         reads=[xs_res[d]], writes=[sq])
                    P.op("tensor", lambda e, d=d: e.matmul(psS1[:, :], lhsT=ones[:, :], rhs=xs[:, d, :],
                                                           start=(d == 0), stop=(d == DC - 1)),
                         reads=[ones, xs_res[d]], writes=[psS1])
                    P.op("tensor", lambda e, d=d, sq=sq: e.matmul(psS2[:, :], lhsT=ones[:, :], rhs=sq[:, :],
                                                                 start=(d == 0), stop=(d == DC - 1)),
                         reads=[ones, sq], writes=[psS2])
            stages.append((ld_o, comp_o))
        run_stages()
        ln_feature_major(P, nc, None, xs, xs_res, DC, TT, ones, psS1, psS2, lng, lnb, sqbuf, st, D)
        o = P.dma("sync", yT[:, t0:t0 + TT].rearrange("(c p) n -> p c n", p=128), xs[:, :, :], reads=xs_res)
        out_ops.append(o)
    return out_ops


RMS_EPS = 1e-6
NH = 16
SM_SCALE = 192 ** -0.5


def build_mla1(nc, P, io, D, T, TT=256):
    DC = D // 128
    NT = T // TT
    NW = TT // 128
    A = lambda name, shape, dt: nc.alloc_sbuf_tensor(uname(name), shape, dt)
    PS = lambda name, shape, dt: nc.alloc_psum_tensor(uname(name), shape, dt)
    xT = io["xT"]
    win = Buf(A("a_win", [128, DC, 1152], BF16))
    wq = Buf(A("a_wq", [128, 4, NH * 256], BF16))
    wkv = Buf(A("a_wkv", [128, 4, 4096], BF16))
    gq = Buf(A("a_gq", [128, 8], F32))
    ones = Buf(A("a_ones", [128, 128], F32))
    xb = Rot([Buf(A(f"a_xb{i}", [128, DC, TT], BF16)) for i in range(2)])
    rope = Rot([Buf(A(f"a_rope{i}", [64, 2, TT], F32)) for i in range(2)])
    cq = A("a_cq", [128, 8, TT], F32)
    cq_res = [Res() for _ in range(8)]
    cqn = A("a_cqn", [128, 8, TT], BF16)
    cqn_res = [Res() for _ in range(8)]
    sq = Rot([Buf(A(f"a_sq{i}", [128, TT], F32)) for i in range(2)])
    rstd = [Buf(A(f"a_rstd{i}", [128, TT], F32)) for i in range(2)]
    rtmp = Rot([Buf(A(f"a_rt{i}", [64, TT], F32)) for i in range(4)])
    big = Rot([Buf(A(f"a_big{i}", [128, NH, TT], BF16)) for i in range(3)])
    krb = Rot([Buf(A(f"a_kr{i}", [64, TT], BF16)) for i in range(2)])
    pool = Rot([Buf(PS(f"a_ps{i}", [128, 512], F32)) for i in range(6)])
    pss = [Buf(PS(f"a_pss{i}", [128, 512], F32)) for i in range(2)]

    for c0 in range(0, 1152, 384):
        P.dma("gpsimd", win[:, :, c0:c0 + 384], io["w_in"][:, c0:c0 + 384].rearrange("(c p) n -> p c n", p=128), writes=[win])
    for c0 in range(0, NH * 256, 1024):
        P.dma("gpsimd", wq[:, :, c0:c0 + 1024], io["w_q"][:, c0:c0 + 1024].rearrange("(c p) n -> p c n", p=128), writes=[wq])
    for c0 in range(0, 4096, 1024):
        P.dma("gpsimd", wkv[:, :, c0:c0 + 1024], io["w_kv"][:, c0:c0 + 1024].rearrange("(c p) n -> p c n", p=128), writes=[wkv])
    P.dma("sync", gq[:, :], io["gq"], writes=[gq])
    P.op("vector", lambda e: e.memset(ones[:, :], 1.0), writes=[ones])

    outs = []
    pend = None

    def load(tt):
        t0 = tt * TT
        xb_ = xb.next()
        rp = rope.next()
        P.dma("gpsimd", xb_[:, :, :], xT[:, t0:t0 + TT].rearrange("(c p) n -> p c n", p=128), writes=[xb_])
        P.dma("sync", rp[:, :, :], io["rope"].rearrange("p (k t) -> p k t", k=2)[:, :, t0:t0 + TT], writes=[rp])
        return xb_, rp

    def rope_combine(pr, prs, rp, out_ap, out_res):
        t1, t2 = rtmp.next(), rtmp.next()
        P.op("vector", lambda e: e.tensor_tensor(out=t1[:, :], in0=pr[0][0:64, pr[1]:pr[1] + TT], in1=rp[:, 0, :], op=ALU.mult),
             reads=[pr[0], rp], writes=[t1])
        P.op("vector", lambda e: e.tensor_tensor(out=t2[:, :], in0=prs[0][0:64, prs[1]:prs[1] + TT], in1=rp[:, 1, :], op=ALU.mult),
             reads=[prs[0], rp], writes=[t2])
        P.op("gpsimd", lambda e: e.tensor_tensor(out=out_ap, in0=t1[:, :], in1=t2[:, :], op=ALU.add),
             reads=[t1, t2], writes=out_res)

    def tile(tt, xb_, rp):
        t0 = tt * TT
        for fc in range(8):
            ps = pool.next()
            for kc in range(DC):
                P.op("tensor", lambda e, ps=ps, kc=kc, fc=fc: e.matmul(
                    ps[:, 0:TT], lhsT=win[:, kc, fc * 128:(fc + 1) * 128], rhs=xb_[:, kc, :],
                    start=(kc == 0), stop=(kc == DC - 1)), reads=[win, xb_], writes=[ps])
            P.op("scalar", lambda e, ps=ps, fc=fc: e.activation(out=cq[:, fc, :], in_=ps[:, 0:TT], func=AF.Copy),
                 reads=[ps], writes=[cq_res[fc]])
            s_ = sq.next()
            P.op("scalar", lambda e, ps=ps, s_=s_: e.activation(out=s_[:, :], in_=ps[:, 0:TT], func=AF.Square),
                 reads=[ps], writes=[s_])
            P.op("tensor", lambda e, s_=s_, fc=fc: e.matmul(pss[fc // 4][:, 0:TT], lhsT=ones[:, :], rhs=s_[:, :],
                                                            start=(fc % 4 == 0), stop=(fc % 4 == 3)),
                 reads=[ones, s_], writes=[pss[fc // 4]])
        prk = pool.next()
        for half in range(2):
            for kc in range(DC):
                P.op("tensor", lambda e, kc=kc, half=half: e.matmul(
                    prk[0:64, half * TT:(half + 1) * TT], lhsT=win[:, kc, 1024 + half * 64:1088 + half * 64], rhs=xb_[:, kc, :],
                    start=(kc == 0), stop=(kc == DC - 1)), reads=[win, xb_], writes=[prk])
        kr_ = krb.next()
        rope_combine((prk, 0), (prk, TT), rp, kr_[:, :], [kr_])
        outs.append(P.dma("sync", io["KR"][:, t0:t0 + TT], kr_[:, :], reads=[kr_]))
        for i in range(2):
            r_ = rstd[i]
            P.op("vector", lambda e, i=i, r_=r_: e.tensor_scalar(out=r_[:, :], in0=pss[i][:, 0:TT], scalar1=1.0 / 512, scalar2=RMS_EPS,
                                                                 op0=ALU.mult, op1=ALU.add), reads=[pss[i]], writes=[r_])
            P.op("scalar", lambda e, r_=r_: e.activation(out=r_[:, :], in_=r_[:, :], func=AF.Sqrt), reads=[r_], writes=[r_])
            P.op("vector", lambda e, r_=r_: e.reciprocal(out=r_[:, :], in_=r_[:, :]), reads=[r_], writes=[r_])
        for fc in range(8):
            P.op("vector", lambda e, fc=fc: e.scalar_tensor_tensor(
                out=cqn[:, fc, :], in0=cq[:, fc, :], scalar=gq[:, fc:fc + 1], in1=rstd[fc // 4][:, :], op0=ALU.mult, op1=ALU.mult),
                reads=[cq_res[fc], gq, rstd[fc // 4]], writes=[cqn_res[fc]])
        qn_, qr_ = big.next(), big.next()
        for h in range(NH):
            pq, pr = pool.next(), pool.next()
            for kc in range(4):
                P.op("tensor", lambda e, pq=pq, kc=kc, h=h: e.matmul(
                    pq[:, 0:TT], lhsT=wq[:, kc, h * 256:h * 256 + 128], rhs=cqn[:, kc, :],
                    start=(kc == 0), stop=(kc == 3)), reads=[wq, cqn_res[kc]], writes=[pq])
            for half in range(2):
                for kc in range(4):
                    P.op("tensor", lambda e, pr=pr, kc=kc, h=h, half=half: e.matmul(
                        pr[0:64, half * TT:(half + 1) * TT], lhsT=wq[:, kc, h * 256 + 128 + half * 64:h * 256 + 192 + half * 64],
                        rhs=cqn[:, kc, :], start=(kc == 0), stop=(kc == 3)), reads=[wq, cqn_res[kc]], writes=[pr])
            P.op("scalar", lambda e, pq=pq, h=h, qn_=qn_: e.activation(out=qn_[:, h, :], in_=pq[:, 0:TT], func=AF.Copy),
                 reads=[pq], writes=[qn_])
            rope_combine((pr, 0), (pr, TT), rp, qr_[0:64, h, :], [qr_])
        outs.append(P.dma("sync", io["Q"][:, :, t0:t0 + TT].rearrange("h p t -> p h t"), qn_[:, :, :], reads=[qn_]))
        outs.append(P.dma("sync", io["QR"][:, :, t0:t0 + TT].rearrange("h p t -> p h t"), qr_[0:64, :, :], reads=[qr_]))
        kn_ = big.next()
        for h in range(NH):
            pk = pool.next()
            for kc in range(4):
                P.op("tensor", lambda e, pk=pk, kc=kc, h=h: e.matmul(
                    pk[:, 0:TT], lhsT=wkv[:, kc, h * 128:(h + 1) * 128], rhs=cqn[:, 4 + kc, :],
                    start=(kc == 0), stop=(kc == 3)), reads=[wkv, cqn_res[4 + kc]], writes=[pk])
            eng = "scalar" if h % 2 == 0 else "vector"
            if eng == "scalar":
                P.op("scalar", lambda e, pk=pk, h=h, kn_=kn_: e.activation(out=kn_[:, h, :], in_=pk[:, 0:TT], func=AF.Copy),
                     reads=[pk], writes=[kn_])
            else:
                P.op("vector", lambda e, pk=pk, h=h, kn_=kn_: e.tensor_copy(out=kn_[:, h, :], in_=pk[:, 0:TT]),
                     reads=[pk], writes=[kn_])
        outs.append(P.dma("sync", io["K"][:, :, t0:t0 + TT].rearrange("h p t -> p h t"), kn_[:, :, :], reads=[kn_]))
        vt_ = big.next()
        vview = vt_.t[:, :, :].rearrange("p h t -> p (h t)").rearrange("p (w f) -> p w f", w=NW)
        for w in range(NW):
            for n in range(4):
                pv = pool.next()
                for kc in range(4):
                    P.op("tensor", lambda e, pv=pv, kc=kc, w=w, n=n: e.matmul(
                        pv[:, :], lhsT=cqn[:, 4 + kc, w * 128:(w + 1) * 128], rhs=wkv[:, kc, 2048 + n * 512:2048 + (n + 1) * 512],
                        start=(kc == 0), stop=(kc == 3)), reads=[wkv, cqn_res[4 + kc]], writes=[pv])
                if n % 2 == 0:
                    P.op("scalar", lambda e, pv=pv, w=w, n=n: e.activation(out=vview[:, w, n * 512:(n + 1) * 512], in_=pv[:, :], func=AF.Copy),
                         reads=[pv], writes=[vt_])
                else:
                    P.op("vector", lambda e, pv=pv, w=w, n=n: e.tensor_copy(out=vview[:, w, n * 512:(n + 1) * 512], in_=pv[:, :]),
                         reads=[pv], writes=[vt_])
        outs.append(P.dma("sync", io["V"][tt * NW:(tt + 1) * NW, :, :].rearrange("w p f -> p w f"), vview, reads=[vt_]))

    for tt in range(NT):
        cur = pend if pend is not None else load(tt)
        pend = load(tt + 1) if tt + 1 < NT else None
        tile(tt, cur[0], cur[1])
    return outs


def build_mla2(nc, P, io, D, T, TT=512, alpha=1.0):
    DC = D // 128
    NT = T // TT
    NB = T // 128
    A = lambda name, shape, dt: nc.alloc_sbuf_tensor(uname(name), shape, dt)
    PS = lambda name, shape, dt: nc.alloc_psum_tensor(uname(name), shape, dt)
    lng = Buf(A("b_lng", [128, DC], F32))
    lnb = Buf(A("b_lnb", [128, DC], F32))
    ones = Buf(A("b_ones", [128, 128], F32))
    onesb = Buf(A("b_onesb", [128, 128], BF16))
    pbias = Buf(A("b_pbias", [128, 1], F32))
    zbias = Buf(A("b_zbias", [128, 1], F32))
    kr = Buf(A("b_kr", [64, 2 * T], BF16))
    kh = Rot([Buf(A(f"b_kh{i}", [128, 2 * T], BF16)) for i in range(2)])
    vh = Rot([Buf(A(f"b_vh{i}", [128, 2 * NB, 128], BF16)) for i in range(2)])
    qh = Rot([Buf(A(f"b_qh{i}", [128, T], BF16)) for i in range(2)])
    qrh = Rot([Buf(A(f"b_qrh{i}", [64, T], BF16)) for i in range(2)])
    oT = A("b_oT", [128, NH, T], BF16)
    oT_res = [[Res() for _ in range(NT)] for _ in range(NH)]
    pT = Rot([Buf(A(f"b_pT{i}", [128, 512], BF16)) for i in range(4)])
    rs = Rot([Buf(A(f"b_rs{i}", [128, 512], F32)) for i in range(2)])
    xs = A("b_xs", [128, DC, TT], F32)
    xs_res = [Res() for _ in range(DC)]
    wo = Rot([Buf(A(f"b_wo{i}", [128, NH, 128], BF16)) for i in range(2)])
    sqbuf = Rot([Buf(A(f"b_sq{i}", [128, TT], F32)) for i in range(2)])
    st = [Buf(A(f"b_st{i}", [128, TT], F32)) for i in range(4)]
    spool = Rot([Buf(PS(f"b_sp{i}", [128, 512], F32)) for i in range(4)])
    apool = Rot([Buf(PS(f"b_ap{i}", [128, 512], F32)) for i in range(4)])

    P.dma("sync", lng[:, :], io["lng"], writes=[lng])
    P.dma("sync", lnb[:, :], io["lnb"], writes=[lnb])
    P.dma("sync", pbias[:, :], io["pbias"], writes=[pbias])
    P.op("vector", lambda e: e.memset(ones[:, :], 1.0), writes=[ones])
    P.op("vector", lambda e: e.memset(onesb[:, :], 1.0), writes=[onesb])
    P.op("vector", lambda e: e.memset(zbias[:, :], 0.0), writes=[zbias])
    P.dma("sync", kr[:, 0:T], io["KRp"], writes=[kr])
    P.dma("sync", kr[:, T:2 * T], io["KRo"], writes=[kr])

    def load_head(h):
        k_, v_, q_, qr_ = kh.next(), vh.next(), qh.next(), qrh.next()
        P.dma("sync", k_[:, 0:T], io["Kp_h"](h), writes=[k_])
        P.dma("sync", k_[:, T:2 * T], io["Ko_h"](h), writes=[k_])
        ng = len(io["Vp_g"])
        bpg = NB // ng
        for g in range(ng):
            P.dma("sync", v_[:, g * bpg:(g + 1) * bpg, :],
                  io["Vp_g"][g][:, h * 128:(h + 1) * 128].rearrange("(b p) f -> p b f", p=128), writes=[v_])
            P.dma("sync", v_[:, NB + g * bpg:NB + (g + 1) * bpg, :],
                  io["Vo_g"][g][:, h * 128:(h + 1) * 128].rearrange("(b p) f -> p b f", p=128), writes=[v_])
        P.dma("sync", q_[:, :], io["Q"][h], writes=[q_])
        P.dma("sync", qr_[:, :], io["QR"][h], writes=[qr_])
        return k_, v_, q_, qr_

    LA = 2

    def qtile(h, qt, k_, v_, q_, qr_):
        po, psm = apool.next(), apool.next()
        nkb = NB + 4 * qt + 4
        its = []

        def qk(kb):
            j = kb - (NB + 4 * qt)
            col0 = j * 128 if j > 0 else 0
            q0 = qt * 512 + col0
            q1 = (qt + 1) * 512
            ps = spool.next()
            P.op("tensor", lambda e: e.matmul(
                ps[:, col0:512], lhsT=k_[:, kb * 128:(kb + 1) * 128], rhs=q_[:, q0:q1], start=True, stop=False),
                reads=[k_, q_], writes=[ps])
            P.op("tensor", lambda e: e.matmul(
                ps[:, col0:512], lhsT=kr[:, kb * 128:(kb + 1) * 128], rhs=qr_[:, q0:q1], start=False, stop=True),
                reads=[kr, qr_], writes=[ps])
            p_ = pT.next()
            bias = pbias if kb < NB else zbias
            P.op("scalar", lambda e: e.activation(
                out=p_[:, col0:512], in_=ps[:, col0:512], func=AF.Exp, bias=bias[:, 0:1], scale=SM_SCALE),
                reads=[ps, bias], writes=[p_])
            if j >= 0:
                P.op("vector", lambda e: e.memset(p_[64:128, j * 128:j * 128 + 64], 0.0), reads=[p_], writes=[p_])
            its.append((p_, col0))

        def pv(kb):
            p_, col0 = its[kb]
            P.op("tensor", lambda e: e.matmul(
                po[:, col0:512], lhsT=v_[:, kb, :], rhs=p_[:, col0:512], start=(kb == 0), stop=(kb == nkb - 1)),
                reads=[v_, p_], writes=[po])
            P.op("tensor", lambda e: e.matmul(
                psm[:, col0:512], lhsT=onesb[:, :], rhs=p_[:, col0:512], start=(kb == 0), stop=(kb == nkb - 1)),
                reads=[onesb, p_], writes=[psm])

        for i in range(nkb + LA):
            if i < nkb:
                qk(i)
            if i >= LA:
                pv(i - LA)
        r_ = rs.next()
        P.op("vector", lambda e: e.reciprocal(out=r_[:, :], in_=psm[:, :]), reads=[psm], writes=[r_])
        P.op("vector", lambda e: e.tensor_tensor(
            out=oT[:, h, qt * 512:(qt + 1) * 512], in0=po[:, :], in1=r_[:, :], op=ALU.mult),
            reads=[po, r_], writes=[oT_res[h][qt]])

    pend = None
    for h in range(NH):
        cur = pend if pend is not None else load_head(h)
        pend = load_head(h + 1) if h + 1 < NH else None
        for qt in range(NT):
            qtile(h, qt, *cur)

    psS1, psS2 = spool.bufs[0], spool.bufs[1]
    opool = Rot(apool.bufs + spool.bufs[2:4])
    xT, w_out, yT = io["xT"], io["w_out"], io["yT"]
    outs = []
    pendw = None

    def ld_wo(d):
        wob = wo.next()
        P.dma("gpsimd", wob[:, :, :], w_out[:, d * 128:(d + 1) * 128].rearrange("(h p) n -> p h n", p=128), writes=[wob])
        return wob

    def p3tile(tt):
        nonlocal pendw
        t0 = tt * TT
        P.dma("sync", xs[:, :, :], xT[:, t0:t0 + TT].rearrange("(c p) n -> p c n", p=128), writes=xs_res)
        for d in range(DC):
            wob = pendw if pendw is not None else ld_wo(d)
            nxt = (tt * DC + d + 1)
            pendw = ld_wo(nxt % DC) if nxt < NT * DC else None
            po = opool.next()
            for h in range(NH):
                P.op("tensor", lambda e, po=po, h=h, wob=wob: e.matmul(
                    po[:, :], lhsT=wob[:, h, :], rhs=oT[:, h, t0:t0 + TT], start=(h == 0), stop=(h == NH - 1)),
                    reads=[wob, oT_res[h][tt]], writes=[po])
            P.op("vector", lambda e, po=po, d=d: e.scalar_tensor_tensor(
                out=xs[:, d, :], in0=xs[:, d, :], scalar=float(alpha), in1=po[:, :], op0=ALU.mult, op1=ALU.add),
                reads=[xs_res[d], po], writes=[xs_res[d]])
            sq = sqbuf.next()
            P.op("scalar", lambda e, sq=sq, d=d: e.activation(out=sq[:, :], in_=xs[:, d, :], func=AF.Square),
                 reads=[xs_res[d]], writes=[sq])
            P.op("tensor", lambda e, d=d: e.matmul(psS1[:, :], lhsT=ones[:, :], rhs=xs[:, d, :],
                                                   start=(d == 0), stop=(d == DC - 1)),
                 reads=[ones, xs_res[d]], writes=[psS1])
            P.op("tensor", lambda e, d=d, sq=sq: e.matmul(psS2[:, :], lhsT=ones[:, :], rhs=sq[:, :],
                                                         start=(d == 0), stop=(d == DC - 1)),
                 reads=[ones, sq], writes=[psS2])
        ln_feature_major(P, nc, None, xs, xs_res, DC, TT, ones, psS1, psS2, lng, lnb, sqbuf, st, D)
        outs.append(P.dma("sync", yT[:, t0:t0 + TT].rearrange("(c p) n -> p c n", p=128), xs[:, :, :], reads=xs_res))
    for tt in range(NT):
        p3tile(tt)
    return outs


D_MODEL = 2048
BATCH = 4
SEQ = 4096
DEPTH = 4
TCORE = 2048
NCORES = 8
GM_HALF = 6144
GM_GROUPS = 8
D_FF = 5504
ALPHA_DN = (2 * DEPTH) ** 0.25
DCH = D_MODEL // 128
NFC = 2 * D_FF // 128
VCH = GM_HALF // 128
PAIRS = [[0, 1], [2, 3], [4, 5], [6, 7]]
_DBG = {}


def build_fused(layers=tuple(range(DEPTH))):
    nc = bass.Bass("TRN2", target_bir_lowering=False, num_devices=NCORES)
    T = TCORE
    D = D_MODEL

    def inp(name, shape, dt=F32):
        return nc.dram_tensor(name, list(shape), dt, kind="ExternalInput").ap()

    def scr(name, shape, dt=F32):
        return nc.dram_tensor(name, list(shape), dt).ap()

    n_gm = max(1, sum(1 for i in layers if i % 2 == 0))
    n_ml = max(1, sum(1 for i in layers if i % 2 == 1))
    n_ff = len(layers)
    gm_k = {i: k for k, i in enumerate([i for i in layers if i % 2 == 0])}
    ml_k = {i: k for k, i in enumerate([i for i in layers if i % 2 == 1])}
    ff_k = {i: k for k, i in enumerate(layers)}
    x_in = inp("xT", (D, T))
    gm_w_in = inp("gm_w_in", (n_gm, D, 2 * GM_HALF))
    gm_gb = inp("gm_gb", (n_gm, 128, 2 * VCH))
    gm_wsT = inp("gm_wsT", (n_gm, 128, GM_GROUPS * 128))
    gm_bsb = inp("gm_bsb", (n_gm, 128, GM_GROUPS * 128))
    gm_w_out = inp("gm_w_out", (n_gm, GM_HALF, D))
    ml_w_in = inp("ml_w_in", (n_ml, D, 1152))
    ml_gq = inp("ml_gq", (n_ml, 128, 8))
    ml_w_q = inp("ml_w_q", (n_ml, 512, NH * 256))
    ml_w_kv = inp("ml_w_kv", (n_ml, 512, 4096))
    ml_w_out = inp("ml_w_out", (n_ml, 2048, D))
    rope = inp("rope", (64, 2 * T))
    pbias = inp("pbias", (128, 1))
    flag = inp("flag", (128, 1))
    f_w_up = inp("f_w_up", (n_ff, D, 2 * D_FF))
    f_cw = inp("f_cw", (n_ff, 128, 3 * NFC))
    f_cb = inp("f_cb", (n_ff, 128, NFC))
    f_w_down = inp("f_w_down", (n_ff, D_FF, D))
    lnm_g = inp("lnm_g", (n_ff, 128, DCH))
    lnm_b = inp("lnm_b", (n_ff, 128, DCH))
    lnf_g = inp("lnf_g", (n_ff, 128, DCH))
    lnf_b = inp("lnf_b", (n_ff, 128, DCH))
    y_out = nc.dram_tensor("yT", [D, T], F32, kind="ExternalOutput").ap()

    actA = scr("actA", (D, T))
    actB = scr("actB", (D, T))
    halo_src = scr("halo_src", (D, 8))
    halo_all = scr("halo_all", (2 * D, 8))
    Qs = scr("Qs", (NH, 128, T), BF16)
    QRs = scr("QRs", (NH, 64, T), BF16)
    Kown = scr("Kown", (NH * 128, T), BF16)
    Kall = [scr(f"Kall{g}", (1024, T), BF16) for g in range(4)]
    KRown = scr("KRown", (64, T), BF16)
    KRall = scr("KRall", (128, T), BF16)
    Vown = scr("Vown", (T, 2048), BF16)
    Vall = [scr(f"Vall{g}", (1024, 2048), BF16) for g in range(4)]

    P = Prog(nc)
    snap = (nc.sbuf_base, nc.psum_base)

    def boundary(renew=False):
        P.full_barrier(renew)
        nc.sbuf_base, nc.psum_base = snap

    def gather(src, dst, kind=8):
        if _DBG.get("nocc") or not (_DBG.get("ccmask", 15) & kind):
            P.dma("sync", dst[0:src.shape[0], :], src)
            return
        P.op("gpsimd", lambda e: e.collective_compute("AllGather", ALU.bypass, replica_groups=PAIRS, ins=[src], outs=[dst]),
             semkey="cc")

    cur = x_in
    outs = []
    for i in layers:
        mid = actA
        fk = ff_k[i]
        if i % 2 == 0:
            slot = gm_k[i]
            io = {"xT": cur, "w_in": gm_w_in[slot], "gb": gm_gb[slot], "wsT": gm_wsT[slot], "bsb": gm_bsb[slot],
                  "w_out": gm_w_out[slot], "lng": lnm_g[fk], "lnb": lnm_b[fk], "yT": mid}
            build_gmlp(nc, P, io, D, GM_HALF, GM_GROUPS, T, alpha=ALPHA_DN)
            boundary(True)
        else:
            slot = ml_k[i]
            io = {"xT": cur, "w_in": ml_w_in[slot], "gq": ml_gq[slot], "w_q": ml_w_q[slot], "w_kv": ml_w_kv[slot], "rope": rope,
                  "Q": Qs, "QR": QRs, "K": Kown.rearrange("(h p) t -> h p t", p=128), "KR": KRown,
                  "V": Vown.rearrange("(b p) f -> b p f", p=128)}
            build_mla1(nc, P, io, D, T)
            boundary()
            for g in range(4):
                gather(Kown[g * 512:(g + 1) * 512, :], Kall[g], 1)
                if _DBG.get("ser", 1):
                    boundary()
                gather(Vown[g * 512:(g + 1) * 512, :], Vall[g], 2)
                if _DBG.get("ser", 1):
                    boundary()
            gather(KRown, KRall, 4)
            boundary()
            io = {"xT": cur, "Q": Qs, "QR": QRs,
                  "Kp_h": (lambda h: Kall[h // 4][(h % 4) * 128:(h % 4) * 128 + 128, :]),
                  "Ko_h": (lambda h: Kown[h * 128:(h + 1) * 128, :]),
                  "KRp": KRall[0:64, :], "KRo": KRown,
                  "Vp_g": [Vall[g][0:512, :] for g in range(4)], "Vo_g": [Vown[g * 512:(g + 1) * 512, :] for g in range(4)],
                  "pbias": pbias, "w_out": ml_w_out[slot], "lng": lnm_g[fk], "lnb": lnm_b[fk], "yT": mid}
            build_mla2(nc, P, io, D, T, alpha=ALPHA_DN)
            boundary(True)
        P.dma("sync", halo_src[:, 0:2], mid[:, T - 2:T])
        boundary()
        gather(halo_src, halo_all)
        boundary()
        dst = y_out if i == layers[-1] else actB
        io = {"xT": mid, "xh": halo_all[0:D, 0:2], "flag": flag, "w_up": f_w_up[fk], "cw": f_cw[fk], "cb": f_cb[fk],
              "w_down": f_w_down[fk], "lng": lnf_g[fk], "lnb": lnf_b[fk], "yT": dst}
        outs = build_ffn(nc, P, io, D, D_FF, T, alpha=ALPHA_DN, ncolblk=256, ndblk=256)
        if i != layers[-1]:
            boundary(True)
        cur = actB
    P.barrier_wait("sync", outs)
    P.emit()
    return nc


def _pc(v, nchunk):
    return np.ascontiguousarray(np.asarray(v, np.float32).reshape(nchunk, 128).T)


def kernel(x, gm_w_in, gm_ln_g, gm_ln_b, gm_w_s, gm_b_s, gm_w_out,
           mla_w_in, mla_q_norm_g, mla_kv_norm_g, mla_w_q_b, mla_w_kv_b, mla_w_out,
           ffn_w_up, ffn_conv_w, ffn_conv_b, ffn_w_down,
           ln_mix_g, ln_mix_b, ln_ffn_g, ln_ffn_b):
    f32 = np.float32
    x = np.asarray(x, f32)
    half = 32
    inv_freq = (f32(10000.0) ** (-np.arange(half, dtype=f32) / f32(half))).astype(f32)
    ang = (np.arange(SEQ, dtype=f32)[:, None] * inv_freq[None, :]).astype(f32)
    cos, sin = np.cos(ang).astype(f32).T, np.sin(ang).astype(f32).T
    ropes = []
    for hf in range(2):
        c_, s_ = cos[:, hf * TCORE:(hf + 1) * TCORE], sin[:, hf * TCORE:(hf + 1) * TCORE]
        ropes.append(np.ascontiguousarray(np.concatenate([np.concatenate([c_, c_], 0), np.concatenate([-s_, s_], 0)], axis=1)))
    st = lambda lst: np.ascontiguousarray(np.stack(lst, 0))
    w_in = np.asarray(mla_w_in, f32)
    wq3 = np.asarray(mla_w_q_b, f32).reshape(2, 512, NH, 192)
    wkv3 = np.asarray(mla_w_kv_b, f32).reshape(2, 512, NH, 256)
    cwt = np.asarray(ffn_conv_w, f32)
    com = {
        "gm_w_in": np.ascontiguousarray(gm_w_in, f32),
        "gm_gb": st([np.concatenate([_pc(gm_ln_g[s], VCH), _pc(gm_ln_b[s], VCH)], axis=1) for s in range(2)]),
        "gm_wsT": st([np.asarray(gm_w_s[s], f32).transpose(2, 0, 1).reshape(128, GM_GROUPS * 128) for s in range(2)]),
        "gm_bsb": st([np.broadcast_to(np.asarray(gm_b_s[s], f32).reshape(1, -1), (128, GM_GROUPS * 128)) for s in range(2)]),
        "gm_w_out": np.ascontiguousarray(gm_w_out, f32),
        "ml_w_in": np.ascontiguousarray(np.concatenate([w_in, w_in[:, :, 1056:1088], w_in[:, :, 1024:1056]], axis=2)),
        "ml_gq": st([np.concatenate([_pc(mla_q_norm_g[s], 4), _pc(mla_kv_norm_g[s], 4)], axis=1) for s in range(2)]),
        "ml_w_q": np.ascontiguousarray(np.concatenate([wq3, wq3[..., 160:192], wq3[..., 128:160]], axis=3).reshape(2, 512, NH * 256)),
        "ml_w_kv": np.ascontiguousarray(np.concatenate([wkv3[..., :128].reshape(2, 512, 2048), wkv3[..., 128:].reshape(2, 512, 2048)], axis=2)),
        "ml_w_out": np.ascontiguousarray(mla_w_out, f32),
        "f_w_up": np.ascontiguousarray(ffn_w_up, f32),
        "f_cw": st([cwt[i].reshape(3, NFC, 128).transpose(2, 0, 1).reshape(128, 3 * NFC) for i in range(DEPTH)]),
        "f_cb": st([_pc(ffn_conv_b[i], NFC) for i in range(DEPTH)]),
        "f_w_down": np.ascontiguousarray(ffn_w_down, f32),
        "lnm_g": st([_pc(ln_mix_g[i], DCH) for i in range(DEPTH)]), "lnm_b": st([_pc(ln_mix_b[i], DCH) for i in range(DEPTH)]),
        "lnf_g": st([_pc(ln_ffn_g[i], DCH) for i in range(DEPTH)]), "lnf_b": st([_pc(ln_ffn_b[i], DCH) for i in range(DEPTH)]),
    }
    in_maps = []
    for c in range(NCORES):
        b, hf = divmod(c, 2)
        m = dict(com)
        m["xT"] = np.ascontiguousarray(x[b, hf * TCORE:(hf + 1) * TCORE, :].T)
        m["rope"] = ropes[hf]
        m["pbias"] = np.full((128, 1), 0.0 if hf == 1 else -30000.0, f32)
        m["flag"] = np.full((128, 1), 1.0 if hf == 1 else 0.0, f32)
        in_maps.append(m)
    nc = build_fused()
    res = run_bass_kernel_spmd(nc, in_maps, core_ids=list(range(NCORES)))
    out = np.empty((BATCH, SEQ, D_MODEL), f32)
    for c in range(NCORES):
        b, hf = divmod(c, 2)
        out[b, hf * TCORE:(hf + 1) * TCORE, :] = res.results[c]["yT"].T
    return out
```
